# Optimizing a Trainium2 kernel written in Bass

```python
import jax, jax.numpy as jnp
from jax import lax
import numpy as np

D_MODEL = 1024
BATCH = 4
SEQ = 4096
DEPTH = 2

D_MIX = D_MODEL
A_WIDTH = D_MIX // 4
A_GROUPS = 4
A_GROUP_DIM = A_WIDTH // A_GROUPS
A_CHUNK = 128
B_WIDTH = D_MIX // 4
B_EXPAND = 64
B_HEADS = B_WIDTH // B_EXPAND
B_KDIM = B_EXPAND
B_VDIM = B_WIDTH // B_HEADS
B_FDIM = B_HEADS * B_KDIM
B_CHUNK = 128
C_WIDTH = D_MIX - A_WIDTH - B_WIDTH
C_HEAD_DIM = 64
C_HEADS = C_WIDTH // C_HEAD_DIM
C_BLOCK = 128
COL_WIDTHS = (A_WIDTH, A_WIDTH, A_WIDTH,
              B_FDIM, B_FDIM, B_WIDTH, B_WIDTH,
              C_WIDTH, C_WIDTH, C_WIDTH, C_WIDTH, C_HEADS)
D_IN = 3 * A_WIDTH + 2 * B_FDIM + 2 * B_WIDTH + 4 * C_WIDTH + C_HEADS
NORM_EPS = 1e-6
F_FLOOR = 1e-30

kernel_name = "hybrid_gmlp_hgrn2_fox_parallel_heads"


def _rmsnorm(x, g):
    xf = x.astype(jnp.float32)
    y = xf * lax.rsqrt(jnp.mean(xf * xf, axis=-1, keepdims=True) + NORM_EPS)
    return (y * g.astype(jnp.float32)).astype(x.dtype)


def _split_cols(proj):
    offsets = []
    acc = 0
    for w in COL_WIDTHS[:-1]:
        acc += w
        offsets.append(acc)
    return jnp.split(proj, offsets, axis=-1)


def _gmlp_mixer(u, v, ln_g, ln_b, w_s, b_s):
    bsz, seq, _ = u.shape
    nc = seq // A_CHUNK
    u = jax.nn.gelu(u)
    v = jax.nn.gelu(v).reshape(bsz, nc, A_CHUNK, A_GROUPS, A_GROUP_DIM)
    vf = v.astype(jnp.float32)
    mu = jnp.mean(vf, axis=-1, keepdims=True)
    var = jnp.mean(jnp.square(vf - mu), axis=-1, keepdims=True)
    vn = (vf - mu) * lax.rsqrt(var + NORM_EPS) * ln_g.astype(jnp.float32) + ln_b.astype(jnp.float32)
    causal = jnp.tril(jnp.ones((A_CHUNK, A_CHUNK), dtype=bool))
    w = jnp.where(causal[None], w_s.astype(jnp.float32), 0.0)
    mixed = jnp.einsum('gts,bnsgc->bntgc', w, vn)
    mixed = mixed + jnp.transpose(b_s.astype(jnp.float32))[None, None, :, :, None]
    return u * mixed.reshape(bsz, seq, A_WIDTH).astype(u.dtype)


def _hgrn2_mixer(q, f_logit, i, lb, onorm_g):
    bsz, seq, _ = q.shape
    nc = seq // B_CHUNK
    qf = jax.nn.silu(q.astype(jnp.float32)) * (B_KDIM ** -0.5)
    z = f_logit.astype(jnp.float32)
    f = lb + (1.0 - lb) * jax.nn.sigmoid(z)
    log_f = jnp.log(jnp.maximum(f, F_FLOOR))
    kf = (1.0 - lb) * jax.nn.sigmoid(-z)
    vf = i.astype(jnp.float32)

    def to_chunks(t, d):
        return t.reshape(bsz, nc, B_CHUNK, B_HEADS, d).transpose(1, 0, 3, 2, 4)

    qc, kc, gc = to_chunks(qf, B_KDIM), to_chunks(kf, B_KDIM), to_chunks(log_f, B_KDIM)
    vc = to_chunks(vf, B_VDIM)
    causal = jnp.tril(jnp.ones((B_CHUNK, B_CHUNK), dtype=bool))[None, None, :, :, None]

    def step(state, inp):
        qx, kx, vx, gx = inp
        b = jnp.cumsum(gx, axis=2)
        o_inter = jnp.einsum('bhtk,bhkv->bhtv', qx * jnp.exp(b), state)
        diff = b[:, :, :, None, :] - b[:, :, None, :, :]
        decay = jnp.exp(jnp.where(causal, diff, -jnp.inf))
        scores = jnp.einsum('bhtk,bhsk,bhtsk->bhts', qx, kx, decay)
        o_intra = jnp.einsum('bhts,bhsv->bhtv', scores, vx)
        b_last = b[:, :, -1:, :]
        new_state = (jnp.exp(b_last[:, :, 0, :])[..., None] * state
                     + jnp.einsum('bhsk,bhsv->bhkv', kx * jnp.exp(b_last - b), vx))
        return new_state, o_inter + o_intra

    state0 = jnp.zeros((bsz, B_HEADS, B_KDIM, B_VDIM), jnp.float32)
    _, ys = lax.scan(step, state0, (qc, kc, vc, gc))
    o = ys.transpose(1, 0, 3, 2, 4).reshape(bsz, seq, B_HEADS, B_VDIM)
    o = o * lax.rsqrt(jnp.mean(o * o, axis=-1, keepdims=True) + NORM_EPS) * onorm_g.astype(jnp.float32)
    return o.reshape(bsz, seq, B_WIDTH).astype(q.dtype)


def _fox_mixer(q, k, v, f_logit, b_f):
    bsz, seq, _ = q.shape

    def heads(t):
        return t.reshape(bsz, seq, C_HEADS, C_HEAD_DIM).transpose(0, 2, 1, 3)

    qh, kh, vh = heads(q), heads(k), heads(v)
    log_f = jax.nn.log_sigmoid(f_logit.astype(jnp.float32) + b_f.astype(jnp.float32))
    c = jnp.cumsum(jnp.transpose(log_f, (0, 2, 1)), axis=-1)
    scale = C_HEAD_DIM ** -0.5
    diag_mask = jnp.tril(jnp.ones((C_BLOCK, C_BLOCK), dtype=bool))
    outs = []
    for blk in range(seq // C_BLOCK):
        q0 = blk * C_BLOCK
        q1 = q0 + C_BLOCK
        s = jnp.einsum('bhqd,bhkd->bhqk', qh[:, :, q0:q1], kh[:, :, :q1]).astype(jnp.float32) * scale
        s = s + c[:, :, q0:q1, None] - c[:, :, None, :q1]
        mask = jnp.concatenate([jnp.ones((C_BLOCK, q0), dtype=bool), diag_mask], axis=1)
        s = jnp.where(mask[None, None], s, -jnp.inf)
        p = jax.nn.softmax(s, axis=-1)
        outs.append(jnp.einsum('bhqk,bhkd->bhqd', p.astype(vh.dtype), vh[:, :, :q1]))
    o = jnp.concatenate(outs, axis=2)
    return o.transpose(0, 2, 1, 3).reshape(bsz, seq, C_WIDTH)


def setup_inputs(seed: int = 0) -> dict:
    key = jax.random.key(seed)
    ks = jax.random.split(key, 13)
    f32 = jnp.float32
    x = jax.random.normal(ks[0], (BATCH, SEQ, D_MODEL), f32)
    norm_g = 1.0 + 0.05 * jax.random.normal(ks[1], (DEPTH, D_MODEL), f32)
    w_in = jax.random.normal(ks[2], (DEPTH, D_MODEL, D_IN), f32) * D_MODEL ** -0.5
    w_out = jax.random.normal(ks[3], (DEPTH, D_MIX, D_MODEL), f32) * (D_MIX ** -0.5) * (2 * DEPTH) ** -0.5
    gmlp_ln_g = 1.0 + 0.05 * jax.random.normal(ks[4], (DEPTH, A_GROUPS, A_GROUP_DIM), f32)
    gmlp_ln_b = 0.02 * jax.random.normal(ks[5], (DEPTH, A_GROUPS, A_GROUP_DIM), f32)
    gmlp_w_s = jax.random.normal(ks[6], (DEPTH, A_GROUPS, A_CHUNK, A_CHUNK), f32) * A_CHUNK ** -0.5
    gmlp_b_s = 1.0 + 0.1 * jax.random.normal(ks[7], (DEPTH, A_GROUPS, A_CHUNK), f32)
    hgrn_lb = 0.1 * jax.random.normal(ks[8], (DEPTH, B_FDIM), f32)
    hgrn_onorm_g = 1.0 + 0.05 * jax.random.normal(ks[9], (DEPTH, B_VDIM), f32)
    fox_b_f = jax.random.uniform(ks[10], (DEPTH, C_HEADS), f32, 0.0, 3.0)
    final_norm_g = 1.0 + 0.05 * jax.random.normal(ks[11], (D_MODEL,), f32)
    return {"x": x, "norm_g": norm_g, "w_in": w_in, "w_out": w_out,
            "gmlp_ln_g": gmlp_ln_g, "gmlp_ln_b": gmlp_ln_b, "gmlp_w_s": gmlp_w_s,
            "gmlp_b_s": gmlp_b_s, "hgrn_lb": hgrn_lb, "hgrn_onorm_g": hgrn_onorm_g,
            "fox_b_f": fox_b_f, "final_norm_g": final_norm_g}


def reference(x, norm_g, w_in, w_out, gmlp_ln_g, gmlp_ln_b, gmlp_w_s, gmlp_b_s,
              hgrn_lb, hgrn_onorm_g, fox_b_f, final_norm_g):
    p = jax.nn.softmax(hgrn_lb.astype(jnp.float32), axis=0)
    lb_all = jnp.clip(jnp.cumsum(p, axis=0) - p[0:1], 0.0, 1.0 - 1e-6)
    for layer in range(DEPTH):
        h = _rmsnorm(x, norm_g[layer])
        proj = jnp.einsum('bsd,de->bse', h, w_in[layer])
        (a_u, a_v, a_z, b_q, b_fl, b_i, b_z,
         c_q, c_k, c_v, c_z, c_fl) = _split_cols(proj)
        y_a = _gmlp_mixer(a_u, a_v, gmlp_ln_g[layer], gmlp_ln_b[layer],
                          gmlp_w_s[layer], gmlp_b_s[layer]) * jax.nn.silu(a_z)
        y_b = _hgrn2_mixer(b_q, b_fl, b_i, lb_all[layer], hgrn_onorm_g[layer]) * jax.nn.silu(b_z)
        y_c = _fox_mixer(c_q, c_k, c_v, c_fl, fox_b_f[layer]) * jax.nn.silu(c_z)
        y = jnp.concatenate([y_a, y_b, y_c], axis=-1)
        x = x + jnp.einsum('bse,ed->bsd', y, w_out[layer])
    return _rmsnorm(x, final_norm_g)
```

```python
from contextlib import ExitStack
from dataclasses import dataclass

import numpy as np
import concourse.bass as bass
import concourse.mybir as mybir
from concourse.bass_utils import run_bass_kernel_spmd

F32 = mybir.dt.float32
BF16 = mybir.dt.bfloat16
AF = mybir.ActivationFunctionType
ALU = mybir.AluOpType

SAME_ENGINE_SYNC = True
EPOCH = 20000
NSLOT = 8


@dataclass(frozen=True)
class Prod:
    eng: str
    is_dma: bool
    semkey: tuple
    val: int


class Buf:
    def __init__(self, name, ap, root=None):
        self.name = name
        self.ap = ap
        self.w = []
        self.r = []
        self.root = root.root if root is not None else self


class Prog:
    ENGS = ("pe", "act", "dve", "pool", "sp")

    def __init__(self, nc):
        self.nc = nc
        self.stack = ExitStack()
        self.ops = {e: [] for e in self.ENGS}
        self.ncomp = {e: 0 for e in self.ENGS}
        self.ndma = {e: 0 for e in self.ENGS}
        self.known = {e: {} for e in self.ENGS}
        self.semkeys = []
        self.finals = []
        self.slot_last = {}
        self.tag = "setup"
        self.namemap = {}

    def sb(self, name, shape, dtype):
        h = self.stack.enter_context(self.nc.sbuf_tensor("sb_" + name, list(shape), dtype))
        return Buf(name, h[:])

    def ps(self, name, shape, dtype=F32):
        h = self.stack.enter_context(self.nc.psum_tensor(name, list(shape), dtype))
        return Buf(name, h[:])

    def _deps(self, eng, reads, writes, me_dma=False):
        raw = []
        for b in reads:
            raw.extend(b.root.w)
        wxx = []
        for b in writes:
            wxx.extend(b.root.w)
            wxx.extend(b.root.r)
        need = {}
        for d in raw:
            if (not d.is_dma) and d.eng == eng and not me_dma and (eng == "pe" or not SAME_ENGINE_SYNC):
                continue
            if need.get(d.semkey, 0) < d.val:
                need[d.semkey] = d.val
        for d in wxx:
            if (not d.is_dma) and d.eng == eng and not me_dma:
                continue
            if need.get(d.semkey, 0) < d.val:
                need[d.semkey] = d.val
        waits = []
        kn = self.known[eng]
        for sk, v in need.items():
            if kn.get(sk, 0) >= v:
                continue
            kn[sk] = v
            waits.append((sk, v))
        return waits

    def _commit(self, me, reads, writes, pwrites):
        for b in reads:
            b.root.r.append(me)
        for b in writes:
            b.root.w = [me]
            b.root.r = []
        for b in pwrites:
            b.root.w = [w for w in b.root.w if w.is_dma or w.eng != me.eng] + [me]
            b.root.r = []

    def capture(self):
        self._cap = []

    def end_capture(self):
        lst, self._cap = self._cap, None
        return lst

    @staticmethod
    def merge_lists(a, b):
        out = []
        na, nb = len(a), len(b)
        ia = ib = 0
        while ia < na or ib < nb:
            if ib >= nb or (ia < na and ia * nb <= ib * na):
                out.append(a[ia]); ia += 1
            else:
                out.append(b[ib]); ib += 1
        return out

    def replay_merged(self, a, b):
        na, nb = len(a), len(b)
        ia = ib = 0
        while ia < na or ib < nb:
            take_a = ib >= nb or (ia < na and ia * nb <= ib * na)
            kind, args, tag = a[ia] if take_a else b[ib]
            if take_a:
                ia += 1
            else:
                ib += 1
            self.tag = tag
            (self.op if kind == "op" else self.dma)(*args)

    def op(self, eng, fn, reads=(), writes=(), pwrites=()):
        if getattr(self, "_cap", None) is not None:
            self._cap.append(("op", (eng, fn, reads, writes, pwrites), self.tag))
            return None
        waits = self._deps(eng, reads, list(writes) + list(pwrites))
        n = self.ncomp[eng]
        self.ncomp[eng] = n + 1
        semkey = ("c", eng, n // EPOCH)
        if semkey not in self.semkeys:
            self.semkeys.append(semkey)
        me = Prod(eng, False, semkey, n % EPOCH + 1)
        self.ops[eng].append((fn, waits, semkey, 1, self.tag))
        self._commit(me, reads, writes, pwrites)
        return me

    def dma(self, eng, fn, reads=(), writes=(), pwrites=(), final=False):
        if getattr(self, "_cap", None) is not None:
            self._cap.append(("dma", (eng, fn, reads, writes, pwrites, final), self.tag))
            return None
        waits = self._deps(eng, reads, list(writes) + list(pwrites), me_dma=True)
        n = self.ndma[eng]
        self.ndma[eng] = n + 1
        slot = n % NSLOT
        semkey = ("d", eng, slot)
        if semkey not in self.semkeys:
            self.semkeys.append(semkey)
        val = 16 * (n // NSLOT + 1)
        kn = self.known[eng]
        if val > 16 and kn.get(semkey, 0) < val - 16:
            kn[semkey] = val - 16
            waits.append((semkey, val - 16))
        me = Prod(eng, True, semkey, val)
        self.ops[eng].append((fn, waits, semkey, 16, self.tag))
        self._commit(me, reads, writes, pwrites)
        if final:
            self.finals.append(me)
        return me

    def emit(self):
        nc = self.nc
        with ExitStack() as st:
            sems = {}
            for k in self.semkeys:
                sems[k] = st.enter_context(nc.semaphore("s_" + "_".join(str(x) for x in k)))

            def replay(eng, e):
                for fn, waits, semkey, inc, tag in self.ops[eng]:
                    for sk, v in waits:
                        e.wait_ge(sems[sk], v)
                    ins = fn(e)
                    ins.then_inc(sems[semkey], inc)
                    try:
                        self.namemap[ins.ins.name] = tag
                    except Exception:
                        pass
                if eng == "sp":
                    done = {}
                    for f in self.finals:
                        done[f.semkey] = max(done.get(f.semkey, 0), f.val)
                    for sk, v in done.items():
                        e.wait_ge(sems[sk], v)

            with nc.Block() as block:
                @block.sync
                def _(e):
                    replay("sp", e)

                @block.scalar
                def _(e):
                    replay("act", e)

                @block.vector
                def _(e):
                    replay("dve", e)

                @block.gpsimd
                def _(e):
                    replay("pool", e)

                @block.tensor
                def _(e):
                    replay("pe", e)
        self.stack.close()


def _consts():
    p = np.arange(128)
    ident = np.eye(128, dtype=np.float32)
    U = (p[:, None] <= p[None, :]).astype(np.float32)
    perm = (p[:, None] == ((p[None, :] + 64) % 128)).astype(np.float32)
    blk = ((p[:, None] // 64) == (p[None, :] // 64)).astype(np.float32)
    rmask = np.ones((128, 512), np.float32)
    rmask[:, ::128] = 0.0
    ubig = np.concatenate([U, np.ones((128, 384), np.float32)], axis=1)
    ones = np.ones((128, 128), np.float32)
    negm = ((U - 1.0) * 30000.0).astype(np.float32)
    return np.ascontiguousarray(np.concatenate([ident, U, perm, blk, rmask, ubig, ones, negm], axis=1))


C_ID, C_U, C_PERM, C_BLK, C_RM, C_UB, C_ONE, C_NEG = 0, 128, 256, 384, 512, 1024, 1536, 1664
NCONST = 1792
FM_COLS = 1408
TM1_C0 = 1408
TM2_C0 = 1920
WCOLS = 1924
EPS = 1e-6


def cap(ap, dims):
    return bass.AP(ap.tensor, ap.offset, [list(ap.ap[0])] + [list(d) for d in dims])


class _Stop(Exception):
    pass


def build_program(SEQ, DEPTH, dbg=False, stop=0):
    nc = bass.Bass("TRN2", target_bir_lowering=False)
    NSB = SEQ // 512
    NCH = SEQ // 128
    D = 1024

    def din(name, shape, dt=F32):
        return nc.dram_tensor(name, list(shape), dt, kind="ExternalInput").ap()

    x_d = din("x", [SEQ, D])
    win_d = din("w_in", [DEPTH * 2 * D, WCOLS])
    wout_d = din("w_out", [DEPTH * D, D])
    cst_d = din("consts", [128, NCONST])
    cols_d = din("cols", [128, 8 * DEPTH])
    ng_d = din("norm_g", [DEPTH, D])
    fg_d = din("fin_g", [1, D])
    lng_d = din("ln_g", [DEPTH * 2, 128])
    lnb_d = din("ln_b", [DEPTH * 2, 128])
    wst_d = din("ws_t", [DEPTH * 2, 2, 128, 128])
    bs_d = din("b_s", [DEPTH * 4, 128])
    bf_d = din("b_f", [DEPTH * 2, 4])
    out_d = nc.dram_tensor("out", [SEQ, D], F32, kind="ExternalOutput").ap()
    x1_d = nc.dram_tensor("x1_scr", [SEQ, D], F32, kind="Internal").ap()
    ht_d = nc.dram_tensor("ht_scr", [NSB, 128, 8 * 512], BF16, kind="Internal").ap()
    yh_d = nc.dram_tensor("yh_scr", [NSB, 128, 4 * 512], BF16, kind="Internal").ap()
    if dbg:
        dbg_d = nc.dram_tensor("dbg", [DEPTH * 2 * NSB, 128, 4 * 512], F32, kind="ExternalOutput").ap()

    P = Prog(nc)
    sb, ps = P.sb, P.ps

    cst = sb("cst", [128, NCONST], F32)
    cols = sb("cols", [128, 8 * DEPTH], F32)
    identb = sb("identb", [128, 128], BF16)
    win = sb("win", [128, 8, WCOLS], BF16)
    WG = [(g * 256, min((g + 1) * 256, WCOLS)) for g in range(8)]
    winG = [Buf(f"win_g{g}", win.ap[:, :, c0:c1]) for g, (c0, c1) in enumerate(WG)]
    wout = sb("wout", [128, 8, D], BF16)
    KTa = [sb(f"KTa{h}", [128, SEQ], BF16) for h in range(4)]
    Vst = sb("Vst", [128, NCH, 384], BF16)
    g_bc = sb("g_bc", [128, D], F32)
    gf_bc = g_bc
    lng_bc = sb("lng_bc", [128, 128], F32)
    lnb_bc = sb("lnb_bc", [128, 128], F32)
    wtf = sb("wtf", [128, 2, 128], F32)
    WTm = sb("WTm", [128, 2, 128], BF16)
    bs_bc = sb("bs_bc", [128, 128], F32)
    bf_bc = sb("bf_bc", [128, 4], F32)
    lbc = sb("lbc", [128, 2], F32)
    epsc = sb("epsc", [128, 1], F32)
    xt = [sb(f"xt{i}", [128, D], F32) for i in range(2)]
    ssq = sb("ssq", [128, 2], F32)
    ssqL = [sb(f"ssqL{i}", [128, 2], F32) for i in range(2)]
    hT = sb("hT", [128, 8, 512], BF16)
    yT = sb("yT", [128, 4, 512], BF16)
    yO = sb("yO", [128, 4, 512], BF16)
    ua = sb("ua", [128, 512], F32)
    sza = sb("sza", [128, 512], F32)
    atmp = sza
    vaL = [sb(f"va{i}", [128, 128], F32) for i in range(4)]
    vsqL = [sb(f"vsq{i}", [128, 128], F32) for i in range(4)]
    lstL = [sb(f"lst{i}", [128, 8], F32) for i in range(4)]
    vnL = [sb(f"vn{i}", [128, 128], BF16) for i in range(4)]
    sq = sb("sq", [128, 512], F32)
    fT = sb("fT", [128, 512], F32)
    lf = sb("lf", [128, 512], F32)
    kf = sb("kf", [128, 512], F32)
    bT = sb("bT", [128, 512], F32)
    dd = [sb(f"dd{i}", [128, 512], F32) for i in range(3)]
    E1 = lf
    Et = [dd[2], fT, dd[0]]
    Etc = [sb(f"Etc{i}", [128, 4, 64], F32) for i in range(2)]
    Q1 = sb("Q1", [128, 512], BF16)
    K1 = sb("K1", [128, 512], BF16)
    QD = sb("QD", [128, 512], BF16)
    KD = sb("KD", [128, 512], BF16)
    QC = sb("QC", [128, 4, 64], BF16)
    KC = sb("KC", [128, 4, 64], BF16)
    K1tok = sb("K1tok", [128, 4, 128], BF16)
    Vb = sb("Vb", [128, 4, 128], BF16)
    scT = [sb(f"scT{i}", [128, 2, 128], BF16) for i in range(4)]
    Sf = sb("Sf", [128, 64], F32)
    Spad = [sb(f"Spad{i}", [128, 5, 64], BF16) for i in range(2)]
    szb = sb("szb", [128, 512], F32)
    osb, osq, rtb = sq, kf, bT
    Qa = [sb(f"Qa{h}", [128, 512], BF16) for h in range(4)]
    szc = [sb(f"szc{i}", [128, 512], F32) for i in range(2)]
    xf = sb("xf", [128, 16], F32)
    nl = sb("nl", [128, 4, 4], F32)
    Cabs = sb("Cabs", [128, NCH, 4], F32)
    carry = sb("carry", [128, NCH + 1, 4], F32)
    biasQ = sb("biasQ", [128, 4, NCH], F32)
    Lp1 = sb("Lp", [128, 4, 128], F32)
    Lp = [Lp1, Lp1]
    pT = [sb(f"pT{i}", [128, 512], BF16) for i in range(4)]
    X = sb("X", [128, 2, 512], F32)
    rl, t1 = dd[1], dd[2]
    junk = Buf("junk", X.ap.rearrange("p a t -> p (a t)"), root=X)
    xr = xt
    xn = [sb(f"xn{i}", [128, D], F32) for i in range(2)]
    hn = [Buf(f"hn{i}", xn[i].ap.bitcast(BF16)[:, 0:D], root=xn[i]) for i in range(2)]
    xo = xn
    Sb = [ps(f"psS{i}", [128, 512]) for i in range(2)]
    Ob = [ps(f"psO{i}", [128, 512]) for i in range(2)]
    Gb = [ps(f"psG{i}", [128, 512]) for i in range(2)]
    Tb = [ps(f"psT{i}", [128, 1024], BF16) for i in range(2)]
    Tf = Buf("psTf", Tb[1].ap.bitcast(F32), root=Tb[1])
    Tf0 = Buf("psTf0", Tb[0].ap.bitcast(F32), root=Tb[0])
    gi = [0]
    ti = [0]

    def G():
        gi[0] += 1
        return Gb[gi[0] % 2]

    def T():
        ti[0] += 1
        return Tb[ti[0] % 2]

    x1_t = [Buf(f"x1_{i}", None) for i in range(NCH)]
    ht_t = [Buf(f"ht_{i}", None) for i in range(NSB)]
    yh_t = [Buf(f"yh_{i}", None) for i in range(NSB)]

    def C(off, n=128):
        return cst.ap[:, off:off + n]

    def mm(out_ap, pairs, reads, writes=(), pwrites=()):
        def fn(e, out_ap=out_ap, pairs=pairs):
            n = len(pairs)
            ins = None
            for i, (l, r) in enumerate(pairs):
                ins = e.matmul(out_ap, lhsT=l, rhs=r, start=(i == 0), stop=(i == n - 1))
            return ins
        P.op("pe", fn, reads, writes, pwrites)

    def act(out, in_, func, reads, writes=(), pwrites=(), **kw):
        P.op("act", lambda e: e.activation(out=out, in_=in_, func=func, **kw), reads, writes, pwrites)

    def tt(eng, out, in0, in1, op, reads, writes=(), pwrites=()):
        P.op(eng, lambda e: e.tensor_tensor(out=out, in0=in0, in1=in1, op=op), reads, writes, pwrites)

    def ts(eng, out, in0, s1, s2, op0, op1, reads, writes=(), pwrites=()):
        if s2 is None:
            P.op(eng, lambda e: e.tensor_scalar(out=out, in0=in0, scalar1=s1, scalar2=None, op0=op0), reads, writes, pwrites)
        else:
            P.op(eng, lambda e: e.tensor_scalar(out=out, in0=in0, scalar1=s1, scalar2=s2, op0=op0, op1=op1), reads, writes, pwrites)

    def stt(eng, out, in0, scalar, in1, op0, op1, reads, writes=(), pwrites=()):
        P.op(eng, lambda e: e.scalar_tensor_tensor(out=out, in0=in0, scalar=scalar, in1=in1, op0=op0, op1=op1), reads, writes, pwrites)

    def cp(eng, out, in_, reads, writes=(), pwrites=()):
        if eng == "act":
            P.op("act", lambda e: e.copy(out=out, in_=in_), reads, writes, pwrites)
        else:
            P.op(eng, lambda e: e.tensor_copy(out=out, in_=in_), reads, writes, pwrites)

    def mset(eng, buf, val, ap=None):
        a = buf.ap if ap is None else ap
        P.op(eng, lambda e: e.memset(a, val), [], [buf] if ap is None else [], [] if ap is None else [buf])

    def bcast_rows(src_ap_row, nparts, n):
        return bass.AP(src_ap_row.tensor, src_ap_row.offset, [[0, nparts], [1, n]])

    P.dma("sp", lambda e: e.dma_start(out=cst.ap, in_=cst_d), [], [cst])
    P.dma("sp", lambda e: e.dma_start(out=cols.ap, in_=cols_d), [], [cols])
    cp("dve", identb.ap, C(C_ID), [cst], [identb])
    mset("dve", epsc, EPS)
    mset("pool", Vst, 1.0)
    for i in range(4):
        mset("pool", scT[i], 0.0)
    mset("pool", Lp1, 0.0)
    for h in range(4):
        mset("pool", KTa[h], 1.0)
        mset("pool", Qa[h], 0.0)

    def load_weights(L, p, do_win=True, do_wout=True):
        r0 = (L * 2 + p) * D
        for g in range(8 if do_win else 0):
            c0, c1 = WG[g]
            P.dma("pool", lambda e, g=g, c0=c0, c1=c1: e.dma_start(
                out=winG[g].ap, in_=win_d[r0:r0 + D, c0:c1].rearrange("(a p) n -> p a n", p=128)), [], [winG[g]])
        if p == 0 and do_wout:
            for et in range(8):
                P.dma("pool", lambda e, et=et: e.dma_start(out=wout.ap[:, et, :], in_=wout_d[L * D + et * 128: L * D + (et + 1) * 128, :]),
                      [], [], [wout])

    def setup_pass(L, p):
        lp = L * 2 + p
        if p == 0:
            P.dma("sp", lambda e: e.dma_start(out=g_bc.ap, in_=bcast_rows(ng_d[L:L + 1, :], 128, D)), [], [g_bc])
        elif L == DEPTH - 1:
            P.dma("sp", lambda e: e.dma_start(out=g_bc.ap, in_=bcast_rows(fg_d[0:1, :], 128, D)), [], [g_bc])
        P.dma("sp", lambda e: e.dma_start(out=lng_bc.ap, in_=bcast_rows(lng_d[lp:lp + 1, :], 128, 128)), [], [lng_bc])
        P.dma("sp", lambda e: e.dma_start(out=lnb_bc.ap, in_=bcast_rows(lnb_d[lp:lp + 1, :], 128, 128)), [], [lnb_bc])
        P.dma("sp", lambda e: e.dma_start(out=bf_bc.ap, in_=bcast_rows(bf_d[lp:lp + 1, :], 128, 4)), [], [bf_bc])
        P.dma("sp", lambda e: e.dma_start(out=wtf.ap, in_=wst_d[lp].rearrange("g s t -> s g t")), [], [wtf])
        for g2 in range(2):
            P.dma("sp", lambda e, g2=g2: e.dma_start(out=bs_bc.ap[g2 * 64:(g2 + 1) * 64, :],
                                                    in_=bcast_rows(bs_d[lp * 2 + g2: lp * 2 + g2 + 1, :], 64, 128)),
                  [], [], [bs_bc])
        tt("dve", WTm.ap, wtf.ap, cap(C(C_U), [[0, 2], [1, 128]]), ALU.mult, [wtf, cst], [WTm])
        if L == 0:
            mset("dve", lbc, 0.0, lbc.ap[:, 0:1])
            mset("dve", lbc, 1.0, lbc.ap[:, 1:2])
        else:
            c0 = cols.ap[:, p:p + 1]
            c1 = cols.ap[:, 8 + p:8 + p + 1]
            tt("dve", lbc.ap[:, 0:1], c1, c0, ALU.subtract, [cols], [lbc])
            act(lbc.ap[:, 0:1], lbc.ap[:, 0:1], AF.Sigmoid, [lbc], [lbc])
            ts("dve", lbc.ap[:, 0:1], lbc.ap[:, 0:1], 1.0 - 1e-6, None, ALU.min, None, [lbc], [lbc])
            ts("dve", lbc.ap[:, 1:2], lbc.ap[:, 0:1], -1.0, 1.0, ALU.mult, ALU.add, [lbc], [lbc])
        mset("dve", Sf, 0.0)
        mset("dve", Spad[0], 0.0)
        mset("dve", Spad[1], 0.0)
        mset("dve", carry, 0.0, carry.ap[:, 0, :])

    ETILE = [[0, 2, 4, 5], [1, 3, 6, 7]]
    pend = [[]]

    def main_loops():
      for L in range(DEPTH):
          xsrc = x_d if L == 0 else x1_d
          last = (L == DEPTH - 1)
          for p in range(2):
              load_weights(L, p, do_win=(L == 0 and p == 0))
              setup_pass(L, p)
              ong = cols.ap[:, 8 * L + 2: 8 * L + 3]
              for sbi in range(NSB):
                  t0 = sbi * 512
                  P.tag = "1hT"
                  def h_phase(sbt):
                      def h_stats(c):
                          ci = sbt * 4 + c
                          xb, hb, sq_ = xt[c % 2], hn[c % 2], ssqL[c % 2]
                          rd = [x1_t[ci]] if L > 0 else []
                          P.dma("sp", lambda e, xb=xb, ci=ci, xsrc=xsrc: e.dma_start(out=xb.ap, in_=xsrc[ci * 128:(ci + 1) * 128, :]), rd, [xb])
                          sc = sq_.ap[:, 0:1]
                          act(junk.ap, xb.ap, AF.Square, [xb], [X, sq_], accum_out=sc)
                          act(sq_.ap[:, 1:2], sc, AF.Sqrt, [sq_, epsc], [sq_], scale=1.0 / D, bias=epsc.ap[:, 0:1])
                          P.op("dve", lambda e, sq_=sq_: e.reciprocal(out=sq_.ap[:, 1:2], in_=sq_.ap[:, 1:2]), [sq_], [sq_])
                          stt("dve", hb.ap, xb.ap, sq_.ap[:, 1:2], g_bc.ap, ALU.mult, ALU.mult, [xb, sq_, g_bc], [hb])

                      def h_xpose(c):
                          hb = hn[c % 2]
                          tb_ = T()
                          for dt in range(8):
                              P.op("pe", lambda e, tb_=tb_, dt=dt, hb=hb: e.transpose(
                                  out=tb_.ap[:, dt * 128:(dt + 1) * 128], in_=hb.ap[:, dt * 128:(dt + 1) * 128], identity=identb.ap),
                                  [hb, identb], [], [tb_])
                          cp("act" if c % 2 == 0 else "dve", hT.ap[:, :, c * 128:(c + 1) * 128],
                             tb_.ap.rearrange("p (k t) -> p k t", k=8), [tb_], [], [hT])

                      h_stats(0)
                      for c in range(4):
                          if c + 1 < 4:
                              h_stats(c + 1)
                          h_xpose(c)
                      P.dma("sp", lambda e, sbt=sbt: e.dma_start(out=ht_d[sbt], in_=hT.ap.rearrange("p a t -> p (a t)")), [hT], [ht_t[sbt]])

                  def h_load(sbt):
                      P.dma("sp", lambda e, sbt=sbt: e.dma_start(out=hT.ap.rearrange("p a t -> p (a t)"), in_=ht_d[sbt]), [ht_t[sbt]], [hT])

                  if sbi == 0 and p == 0:
                      h_phase(0)

                  if stop == 1:
                      raise _Stop()
                  P.tag = "2FM"
                  P.capture()
                  def fm(ft):
                      g = G()
                      mm(g.ap, [(win.ap[:, dt, ft * 128:(ft + 1) * 128], hT.ap[:, dt, :]) for dt in range(8)], [winG[ft // 2], hT], [g])
                      return g

                  g = fm(0); act(ua.ap, g.ap, AF.Gelu_apprx_tanh, [g], [ua])
                  g = fm(1); act(sza.ap, g.ap, AF.Silu, [g], [sza])
                  tt("pool", ua.ap, ua.ap, sza.ap, ALU.mult, [ua, sza], [ua])
                  g = fm(2); act(sq.ap, g.ap, AF.Silu, [g], [sq])
                  g = fm(3); act(fT.ap, g.ap, AF.Sigmoid, [g], [fT])
                  g = fm(4); act(szb.ap, g.ap, AF.Silu, [g], [szb])
                  psq = [None, None]
                  for pr in range(2):
                      g = fm(5 + pr)
                      P.op("act", lambda e, g=g, pr=pr: e.mul(out=Qa[2 * pr].ap[0:64, :], in_=g.ap[0:64, :], mul=0.125), [g], [], [Qa[2 * pr]])
                      ts("dve", Qa[2 * pr + 1].ap[64:128, :], g.ap[64:128, :], 0.125, None, ALU.mult, None, [g], [], [Qa[2 * pr + 1]])
                  for pr in range(2):
                      g = fm(7 + pr)
                      cp("act", KTa[2 * pr].ap[0:64, t0:t0 + 512], g.ap[0:64, :], [g], [], [KTa[2 * pr]])
                      cp("dve", KTa[2 * pr + 1].ap[64:128, t0:t0 + 512], g.ap[64:128, :], [g], [], [KTa[2 * pr + 1]])
                  for pr in range(2):
                      g = fm(9 + pr); act(szc[pr].ap, g.ap, AF.Silu, [g], [szc[pr]])

                  listFM = P.end_capture()
                  P.tag = "3TMA"
                  P.capture()
                  psA = Ob[0]
                  psF = Tf
                  for c in range(4):
                      ci = sbi * 4 + c
                      va = vaL[c]
                      g = [Sb[0], Sb[1], Tf0, Sb[0]][c]
                      mm(g.ap, [(hT.ap[:, dt, c * 128:(c + 1) * 128], win.ap[:, dt, TM1_C0:TM1_C0 + 512]) for dt in range(8)], [winG[5], winG[6], winG[7], hT], [g])
                      act(va.ap, g.ap[:, 0:128], AF.Gelu_apprx_tanh, [g], [va])
                      cp("dve", Vb.ap[:, c, :], g.ap[:, 128:256], [g, va], [], [Vb])
                      for pr in range(2):
                          cp("dve",
                             cap(Vst.ap[:, ci, pr * 192:pr * 192 + 1], [[128, 2], [1, 64]]),
                             g.ap[:, 256 + pr * 128: 256 + (pr + 1) * 128].rearrange("p (h d) -> p h d", h=2), [g, va], [], [Vst])
                      mm(psF.ap[:, c * 4:(c + 1) * 4],
                         [(hT.ap[:, dt, c * 128:(c + 1) * 128], win.ap[:, dt, TM2_C0:TM2_C0 + 4]) for dt in range(8)], [winG[7], hT], [], [psF])

                  listTM = P.end_capture()
                  P.capture()

                  def ln_stage(s, c):
                      va, vsq, lst, vn = vaL[c], vsqL[c], lstL[c], vnL[c]
                      vc = vsq
                      va3 = va.ap.rearrange("p (g d) -> p g d", g=2)
                      vc3 = vc.ap.rearrange("p (g d) -> p g d", g=2)
                      if s == 0:
                          P.op("dve", lambda e, va3=va3, lst=lst: e.tensor_reduce(out=lst.ap[:, 0:2], in_=va3, axis=mybir.AxisListType.X, op=ALU.add),
                               [va], [], [lst])
                          tt("pool", vsq.ap, va.ap, va.ap, ALU.mult, [va], [vsq])
                      elif s == 1:
                          P.op("dve", lambda e, vsq=vsq, lst=lst: e.tensor_reduce(out=lst.ap[:, 2:4], in_=vsq.ap.rearrange("p (g d) -> p g d", g=2),
                                                                                  axis=mybir.AxisListType.X, op=ALU.add), [vsq], [], [lst])
                          ts("dve", lst.ap[:, 4:6], lst.ap[:, 0:2], 1.0 / 64, None, ALU.mult, None, [lst], [], [lst])
                      elif s == 2:
                          tt("dve", lst.ap[:, 0:2], lst.ap[:, 4:6], lst.ap[:, 4:6], ALU.mult, [lst], [], [lst])
                      elif s == 3:
                          stt("dve", lst.ap[:, 6:8], lst.ap[:, 2:4], 1.0 / 64, lst.ap[:, 0:2], ALU.mult, ALU.subtract, [lst], [], [lst])
                      elif s == 4:
                          act(lst.ap[:, 6:8], lst.ap[:, 6:8], AF.Sqrt, [lst, epsc], [], [lst], bias=epsc.ap[:, 0:1])
                      elif s == 5:
                          P.op("dve", lambda e, lst=lst: e.reciprocal(out=lst.ap[:, 6:8], in_=lst.ap[:, 6:8]), [lst], [], [lst])
                      elif s == 6:
                          tt("dve", vc3, va3, cap(lst.ap[:, 4:5], [[1, 2], [0, 64]]), ALU.subtract, [va, lst], [vc])
                      elif s == 7:
                          tt("dve", vc3, vc3, cap(lst.ap[:, 6:7], [[1, 2], [0, 64]]), ALU.mult, [vc, lst], [vc])
                      elif s == 8:
                          tt("pool", vc.ap, vc.ap, lng_bc.ap, ALU.mult, [vc, lng_bc], [vc])
                      elif s == 9:
                          tt("pool", vn.ap, vc.ap, lnb_bc.ap, ALU.add, [vc, lnb_bc], [vn])
                      elif s == 10:
                          for g2 in range(2):
                              mm(psA.ap[g2 * 64:(g2 + 1) * 64, c * 128:(c + 1) * 128],
                                 [(vn.ap[:, g2 * 64:(g2 + 1) * 64], WTm.ap[:, g2, :])], [vn, WTm], [], [psA])

                  for s in range(11):
                      for c in range(4):
                          ln_stage(s, c)
                  tt("dve", atmp.ap.rearrange("p (c t) -> p c t", c=4), psA.ap.rearrange("p (c t) -> p c t", c=4),
                     cap(bs_bc.ap[:, 0:1], [[0, 4], [1, 128]]), ALU.add, [psA, bs_bc], [atmp])
                  tt("pool", yT.ap[:, 0, :], atmp.ap, ua.ap, ALU.mult, [atmp, ua], [], [yT])

                  list3 = P.end_capture()
                  P.tag = "5aCset"
                  P.capture()
                  tt("dve", xf.ap.rearrange("p (c h) -> p c h", c=4), psF.ap[:, 0:16].rearrange("p (c h) -> p c h", c=4),
                     cap(bf_bc.ap[:, 0:1], [[0, 4], [1, 4]]), ALU.add, [psF, bf_bc], [xf])
                  act(xf.ap, xf.ap, AF.Exp, [xf], [xf], scale=-1.0)
                  act(nl.ap.rearrange("p c h -> p (c h)"), xf.ap, AF.Ln, [xf], [nl], bias=1.0)
                  gC = Sb[0]
                  mm(gC.ap[:, 0:16], [(C(C_U), nl.ap.rearrange("p c h -> p (c h)"))], [cst, nl], [gC])
                  mm(gC.ap[:, 16:32], [(C(C_ONE), nl.ap.rearrange("p c h -> p (c h)"))], [cst, nl], [], [gC])
                  for c in range(4):
                      ci = sbi * 4 + c
                      tt("dve", Cabs.ap[:, ci, :], gC.ap[:, c * 4:(c + 1) * 4], carry.ap[:, ci, :], ALU.add, [gC, carry], [], [Cabs])
                      tt("dve", carry.ap[:, ci + 1, :], gC.ap[:, 16 + c * 4:16 + (c + 1) * 4], carry.ap[:, ci, :], ALU.add, [gC, carry], [], [carry])
                  nkb = 4 * sbi + 4
                  for hd in range(4):
                      ts("dve", biasQ.ap[:, hd, 0:nkb], cap(Cabs.ap[:, 0, hd:hd + 1], [[4, nkb]]), carry.ap[:, 4 * sbi, hd:hd + 1], None,
                         ALU.subtract, None, [Cabs, carry], [], [biasQ])
                  for pr in range(2):
                      hA, hB = 2 * pr, 2 * pr + 1
                      cp("pool", Lp[pr].ap[:, :, 64:65], nl.ap[:, :, hA:hA + 1], [nl], [], [Lp[pr]])
                      cp("pool", Lp[pr].ap[:, :, 0:1], nl.ap[:, :, hB:hB + 1], [nl], [], [Lp[pr]])
                      gR = Sb[1] if pr == 0 else Gb[0]
                      def fnr(e, gR=gR, pr=pr):
                          ins = None
                          for c in range(4):
                              ins = e.matmul(gR.ap[:, c * 128:512], lhsT=Lp[pr].ap[:, c, :], rhs=C(C_UB, 512 - c * 128),
                                             start=(c == 0), stop=(c == 3))
                          return ins
                      P.op("pe", fnr, [Lp[pr], cst], [gR])
                      ts("dve", Qa[hA].ap[64:128, :], gR.ap[64:128, :], -1.0, None, ALU.mult, None, [gR], [], [Qa[hA]])
                      ts("dve", Qa[hB].ap[0:64, :], gR.ap[0:64, :], -1.0, None, ALU.mult, None, [gR], [], [Qa[hB]])
                  listCs = P.end_capture()
                  P.tag = "4B"
                  P.capture()
                  ts("dve", fT.ap, fT.ap, lbc.ap[:, 1:2], lbc.ap[:, 0:1], ALU.mult, ALU.add, [fT, lbc], [fT])
                  ts("dve", fT.ap, fT.ap, 1e-30, None, ALU.max, None, [fT], [fT])
                  act(lf.ap, fT.ap, AF.Ln, [fT], [lf])
                  ts("pool", kf.ap, fT.ap, -1.0, 1.0, ALU.mult, ALU.add, [fT], [kf])
                  P.op("dve", lambda e: e.tensor_tensor_scan(out=bT.ap, data0=C(C_RM, 512), data1=lf.ap, initial=0.0,
                                                             op0=ALU.mult, op1=ALU.add), [lf, cst], [bT])
                  b8 = bT.ap.rearrange("p (a t) -> p a t", a=8)
                  b4 = bT.ap.rearrange("p (a t) -> p a t", a=4)
                  tt("dve", dd[0].ap.rearrange("p (a t) -> p a t", a=8), b8, cap(bT.ap[:, 31:32], [[64, 8], [0, 64]]), ALU.subtract, [bT], [dd[0]])
                  tt("dve", dd[1].ap.rearrange("p (a t) -> p a t", a=4), b4, cap(bT.ap[:, 63:64], [[128, 4], [0, 128]]), ALU.subtract, [bT], [dd[1]])
                  tt("pool", dd[2].ap.rearrange("p (a t) -> p a t", a=4), b4, cap(bT.ap[:, 127:128], [[128, 4], [0, 128]]), ALU.subtract, [bT], [dd[2]])
                  act(E1.ap, bT.ap, AF.Exp, [bT], [E1])
                  act(Et[0].ap, dd[2].ap, AF.Exp, [dd[2]], [Et[0]], scale=-1.0)
                  act(Et[1].ap, dd[0].ap, AF.Exp, [dd[0]], [Et[1]])
                  act(Et[2].ap, dd[0].ap, AF.Exp, [dd[0]], [Et[2]], scale=-1.0)
                  d14 = dd[1].ap.rearrange("p (a t) -> p a t", a=4)
                  act(Etc[0].ap, d14[:, :, 64:128], AF.Exp, [dd[1]], [Etc[0]])
                  act(Etc[1].ap, d14[:, :, 0:64], AF.Exp, [dd[1]], [Etc[1]], scale=-1.0)
                  sq4 = sq.ap.rearrange("p (a t) -> p a t", a=4)
                  kf4 = kf.ap.rearrange("p (a t) -> p a t", a=4)
                  tt("dve", Q1.ap, sq.ap, E1.ap, ALU.mult, [sq, E1], [Q1])
                  tt("pool", K1.ap, kf.ap, Et[0].ap, ALU.mult, [kf, Et[0]], [K1])
                  tt("dve", QD.ap, sq.ap, Et[1].ap, ALU.mult, [sq, Et[1]], [QD])
                  tt("pool", KD.ap, kf.ap, Et[2].ap, ALU.mult, [kf, Et[2]], [KD])
                  tt("dve", QC.ap, sq4[:, :, 64:128], Etc[0].ap, ALU.mult, [sq, Etc[0]], [QC])
                  tt("pool", KC.ap, kf4[:, :, 0:64], Etc[1].ap, ALU.mult, [kf, Etc[1]], [KC])
                  tb_ = Tb[1]
                  for c in range(4):
                      P.op("pe", lambda e, c=c, tb_=tb_: e.transpose(out=tb_.ap[:, c * 128:(c + 1) * 128], in_=K1.ap[:, c * 128:(c + 1) * 128],
                                                                    identity=identb.ap), [K1, identb], [], [tb_])
                  cp("act", K1tok.ap.rearrange("p c k -> p (c k)"), tb_.ap[:, 0:512], [tb_], [K1tok])
                  listB1 = P.end_capture()
                  P.tag = "4B"
                  psO = Tf
                  psS = Tf
                  sbank = [[Gb[0], Gb[0]], [Gb[1], Gb[1]]]
                  P.capture()
                  for c in range(4):
                      st_ = scT[c]
                      for h2 in range(2):
                          sc_ps = sbank[h2][c % 2]
                          r = slice(h2 * 64, (h2 + 1) * 64)
                          mm(sc_ps.ap[0:64, 0:64], [(KD.ap[r, c * 128:c * 128 + 64], QD.ap[r, c * 128:c * 128 + 64])], [KD, QD], [sc_ps])
                          mm(sc_ps.ap[0:64, 64:128], [(KC.ap[r, c, :], QC.ap[r, c, :])], [KC, QC], [], [sc_ps])
                          mm(sc_ps.ap[64:128, 64:128], [(KD.ap[r, c * 128 + 64:c * 128 + 128], QD.ap[r, c * 128 + 64:c * 128 + 128])],
                             [KD, QD], [], [sc_ps])
                          tt("dve", st_.ap[0:64, h2, :], sc_ps.ap[0:64, 0:128], cst.ap[0:64, C_U:C_U + 128], ALU.mult, [sc_ps, cst], [], [st_])
                          tt("dve", st_.ap[64:128, h2, 64:128], sc_ps.ap[64:128, 64:128], cst.ap[64:128, C_U + 64:C_U + 128], ALU.mult,
                             [sc_ps, cst], [], [st_])
                  for c in range(4):
                      for h2 in range(2):
                          r = slice(h2 * 64, (h2 + 1) * 64)
                          mm(psS.ap[r, c * 64:(c + 1) * 64], [(K1tok.ap[:, c, r], Vb.ap[:, c, r])], [K1tok, Vb], [], [psS])
                  for c in range(4):
                      stt("dve", Sf.ap, Sf.ap, E1.ap[:, c * 128 + 127:c * 128 + 128], psS.ap[:, c * 64:(c + 1) * 64], ALU.mult, ALU.add,
                          [Sf, E1, psS], [Sf])
                      cp("pool", Spad[0].ap[0:64, c + 1, :], Sf.ap[0:64, :], [Sf], [], [Spad[0]])
                      cp("pool", Spad[1].ap[64:128, c + 1, :], Sf.ap[64:128, :], [Sf], [], [Spad[1]])
                  listB2a = P.end_capture()
                  P.capture()
                  for c in range(4):
                      for h2 in range(2):
                          r = slice(h2 * 64, (h2 + 1) * 64)
                          mm(psO.ap[r, c * 128:(c + 1) * 128],
                             [(Vb.ap[:, c, h2 * 64:(h2 + 1) * 64], scT[c].ap[:, h2, :]),
                              (Spad[h2].ap[:, c, :], Q1.ap[:, c * 128:(c + 1) * 128])], [Vb, scT[c], Spad[h2], Q1], [], [psO])
                  cp("pool", Spad[0].ap[0:64, 0, :], Spad[0].ap[0:64, 4, :], [Spad[0]], [], [Spad[0]])
                  cp("pool", Spad[1].ap[64:128, 0, :], Spad[1].ap[64:128, 4, :], [Spad[1]], [], [Spad[1]])
                  ts("dve", osb.ap, psO.ap, 0.125, None, ALU.mult, None, [psO], [osb])
                  tt("pool", osq.ap, osb.ap, osb.ap, ALU.mult, [osb], [osq])
                  g = Tf
                  mm(g.ap, [(C(C_BLK), osq.ap)], [cst, osq], [g])
                  act(rtb.ap, g.ap, AF.Sqrt, [g, epsc], [rtb], scale=1.0 / 64, bias=epsc.ap[:, 0:1])
                  P.op("dve", lambda e: e.reciprocal(out=rtb.ap, in_=rtb.ap), [rtb], [rtb])
                  stt("dve", osb.ap, osb.ap, ong, rtb.ap, ALU.mult, ALU.mult, [osb, cols, rtb], [osb])
                  tt("pool", yT.ap[:, 1, :], osb.ap, szb.ap, ALU.mult, [osb, szb], [], [yT])
                  listB2b = P.end_capture()
                  P.tag = "5C"
                  Sall = [Sb[0], Sb[1], Tf0]
                  NS, LA = 3, 2
                  P.capture()
                  items = []
                  for pr in range(2):
                      for h2 in range(2):
                          for kb in range(nkb):
                              items.append((pr, h2, kb))

                  def emit_score(i):
                      pr, h2, kb = items[i]
                      hd = 2 * pr + h2
                      q0 = max(0, kb - 4 * sbi) * 128
                      s_ = Sall[i % NS]
                      mm(s_.ap[:, q0:512], [(KTa[hd].ap[:, kb * 128:(kb + 1) * 128], Qa[hd].ap[:, q0:512])], [KTa[hd], Qa[hd]], [s_])
                      if kb >= 4 * sbi:
                          tt("dve", s_.ap[:, q0:q0 + 128], s_.ap[:, q0:q0 + 128], C(C_NEG), ALU.add, [s_, cst], [], [s_])

                  def emit_rest(i):
                      pr, h2, kb = items[i]
                      hd = 2 * pr + h2
                      q0 = max(0, kb - 4 * sbi) * 128
                      s_ = Sall[i % NS]
                      p_ = pT[i % 4]
                      ob = Ob[h2]
                      lo = pr * 192 + (0 if h2 == 0 else 64)
                      act(p_.ap[:, q0:512], s_.ap[:, q0:512], AF.Exp, [s_, biasQ], [p_], bias=biasQ.ap[:, hd, kb:kb + 1])

                      def fnpv(e, ob=ob, q0=q0, kb=kb, lo=lo, p_=p_, nkb=nkb):
                          return e.matmul(ob.ap[:, q0:512], lhsT=Vst.ap[:, kb, lo:lo + 128], rhs=p_.ap[:, q0:512],
                                          start=(kb == 0), stop=(kb == nkb - 1))
                      P.op("pe", fnpv, [Vst, p_], [ob] if kb == 0 else [], [] if kb == 0 else [ob])

                  def finalize(pr, i_last):
                      cp("dve", X.ap[:, 0, :], Ob[0].ap, [Ob[0]], [], [X])
                      cp("dve", X.ap[:, 1, :], Ob[1].ap, [Ob[1]], [], [X])
                      gW = Sall[i_last % NS]
                      mm(gW.ap[0:64, :], [(C(C_PERM, 64), X.ap[:, 0, :])], [cst, X], [gW])
                      mm(gW.ap[64:128, :], [(cst.ap[:, C_PERM + 64:C_PERM + 128], X.ap[:, 1, :])], [cst, X], [], [gW])
                      P.op("dve", lambda e, gW=gW: e.reciprocal(out=gW.ap, in_=gW.ap), [gW], [gW])
                      tt("dve", X.ap[0:64, 0, :], X.ap[0:64, 0, :], gW.ap[0:64, :], ALU.mult, [X, gW], [], [X])
                      tt("dve", X.ap[64:128, 1, :], X.ap[64:128, 1, :], gW.ap[64:128, :], ALU.mult, [X, gW], [], [X])
                      tt("pool", yT.ap[0:64, 2 + pr, :], X.ap[0:64, 0, :], szc[pr].ap[0:64, :], ALU.mult, [X, szc[pr]], [], [yT])
                      tt("pool", yT.ap[64:128, 2 + pr, :], X.ap[64:128, 1, :], szc[pr].ap[64:128, :], ALU.mult, [X, szc[pr]], [], [yT])

                  for j in range(min(LA, len(items))):
                      emit_score(j)
                  for i in range(len(items)):
                      if i + LA < len(items):
                          emit_score(i + LA)
                      emit_rest(i)
                      pr, h2, kb = items[i]
                      if h2 == 1 and kb == nkb - 1:
                          finalize(pr, i)
                  listC2 = P.end_capture()
                  if sbi == 0:
                      P.replay_merged(listFM, pend[0])
                      P.replay_merged(listTM, [])
                  else:
                      P.replay_merged(P.merge_lists(listFM, listTM), pend[0])
                  pend[0] = []
                  if p == 1:
                      P.tag = "6out"
                      P.dma("sp", lambda e, sbi=sbi: e.dma_start(out=yO.ap.rearrange("p a t -> p (a t)"), in_=yh_d[sbi]), [yh_t[sbi]], [yO])
                  listH = []
                  if sbi + 1 < NSB:
                      P.tag = "1hT"
                      if p == 0:
                          P.capture()
                          h_phase(sbi + 1)
                          listH = P.end_capture()
                      else:
                          h_load(sbi + 1)
                  else:
                      nxt = (L, 1) if p == 0 else ((L + 1, 0) if L + 1 < DEPTH else None)
                      if nxt is not None:
                          load_weights(nxt[0], nxt[1], do_win=True, do_wout=False)
                      if p == 0:
                          P.tag = "1hT"
                          h_load(0)
                  P.replay_merged(P.merge_lists(list3, listCs), listH)
                  _h = min(len(listC2), 24)
                  P.replay_merged(listC2[:_h], [])
                  P.replay_merged(listC2[_h:], listB1 + listB2a + listB2b)
                  P.tag = "5C"

                  if dbg:
                      P.op("pool", lambda e: e.tensor_copy(out=X.ap.rearrange("p a t -> p (a t)"), in_=yT.ap[:, 0:2, :].rearrange("p a t -> p (a t)")), [yT], [X])
                      di = (L * 2 + p) * NSB + sbi
                      P.dma("sp", lambda e, di=di: e.dma_start(out=dbg_d[di][:, 0:1024], in_=X.ap.rearrange("p a t -> p (a t)")), [X], [], final=True)
                      P.op("pool", lambda e: e.tensor_copy(out=X.ap.rearrange("p a t -> p (a t)"), in_=yT.ap[:, 2:4, :].rearrange("p a t -> p (a t)")), [yT], [X])
                      P.dma("sp", lambda e, di=di: e.dma_start(out=dbg_d[di][:, 1024:2048], in_=X.ap.rearrange("p a t -> p (a t)")), [X], [], final=True)

                  if stop == 5:
                      raise _Stop()
                  P.tag = "6out"
                  if p == 0:
                      P.dma("sp", lambda e, sbi=sbi: e.dma_start(out=yh_d[sbi], in_=yT.ap.rearrange("p a t -> p (a t)")), [yT], [yh_t[sbi]])
                  else:
                      P.capture()
                      for c in range(4):
                          ci = sbi * 4 + c
                          xb, xnb, xob = xr[c % 2], xn[c % 2], xo[c % 2]
                          rd = [x1_t[ci]] if L > 0 else []
                          P.dma("act", lambda e, xb=xb, ci=ci, xsrc=xsrc: e.dma_start(out=xb.ap, in_=xsrc[ci * 128:(ci + 1) * 128, :]), rd, [xb])
                          for dh in range(2):
                              g = Ob[dh]
                              pairs = []
                              for half, ysrc in ((0, yO), (1, yT)):
                                  for i in range(4):
                                      et = ETILE[half][i]
                                      pairs.append((ysrc.ap[:, i, c * 128:(c + 1) * 128], wout.ap[:, et, dh * 512:(dh + 1) * 512]))
                              mm(g.ap, pairs, [yO, yT, wout], [g])
                              tt("dve", xnb.ap[:, dh * 512:(dh + 1) * 512], g.ap, xb.ap[:, dh * 512:(dh + 1) * 512], ALU.add, [g, xb], [], [xnb])
                          if not last:
                              P.dma("sp", lambda e, xnb=xnb, ci=ci: e.dma_start(out=x1_d[ci * 128:(ci + 1) * 128, :], in_=xnb.ap), [xnb], [x1_t[ci]])
                          else:
                              sq_ = ssqL[c % 2]
                              sc = sq_.ap[:, 0:1]
                              act(junk.ap, xnb.ap, AF.Square, [xnb], [X, sq_], accum_out=sc)
                              act(sq_.ap[:, 1:2], sc, AF.Sqrt, [sq_, epsc], [sq_], scale=1.0 / D, bias=epsc.ap[:, 0:1])
                              P.op("dve", lambda e, sq_=sq_: e.reciprocal(out=sq_.ap[:, 1:2], in_=sq_.ap[:, 1:2]), [sq_], [sq_])
                              stt("dve", xob.ap, xnb.ap, sq_.ap[:, 1:2], gf_bc.ap, ALU.mult, ALU.mult, [xnb, sq_, gf_bc], [xob])
                              P.dma("sp", lambda e, xob=xob, ci=ci: e.dma_start(out=out_d[ci * 128:(ci + 1) * 128, :], in_=xob.ap), [xob], [], final=True)
                      pend[0] = P.end_capture()
              P.replay_merged(pend[0], [])
              pend[0] = []
    try:
        main_loops()
    except _Stop:
        pass
    P.emit()
    nc._namemap = P.namemap
    return nc


def _prep_shared(norm_g, w_in, w_out, gmlp_ln_g, gmlp_ln_b, gmlp_w_s, gmlp_b_s,
                 hgrn_lb, hgrn_onorm_g, fox_b_f, final_norm_g):
    DEPTH = w_in.shape[0]
    f = np.float32
    o = {}
    A_U, A_V, A_Z, B_Q, B_FL, B_I, B_Z, C_Q, C_K, C_V, C_Z, C_FL = 0, 256, 512, 768, 1024, 1280, 1536, 1792, 2304, 2816, 3328, 3840
    win = np.empty((DEPTH, 2, 1024, WCOLS), f)
    for p in range(2):
        a = slice(p * 128, (p + 1) * 128)
        c = slice(p * 256, (p + 1) * 256)
        idx = np.concatenate([
            np.arange(A_U, A_U + 256)[a], np.arange(A_Z, A_Z + 256)[a],
            np.arange(B_Q, B_Q + 256)[a], np.arange(B_FL, B_FL + 256)[a], np.arange(B_Z, B_Z + 256)[a],
            np.arange(C_Q, C_Q + 512)[c], np.arange(C_K, C_K + 512)[c], np.arange(C_Z, C_Z + 512)[c],
            np.arange(A_V, A_V + 256)[a], np.arange(B_I, B_I + 256)[a], np.arange(C_V, C_V + 512)[c],
            np.arange(C_FL, C_FL + 8)[p * 4:(p + 1) * 4]])
        assert idx.size == WCOLS
        win[:, p] = w_in[:, :, idx]
    o["w_in"] = np.ascontiguousarray(win.reshape(DEPTH * 2 * 1024, WCOLS))
    o["w_out"] = np.ascontiguousarray(w_out.reshape(DEPTH * 1024, 1024).astype(f))
    o["consts"] = _consts()
    cols = np.zeros((128, 8 * DEPTH), f)
    for L in range(DEPTH):
        cols[:, 8 * L + 0] = hgrn_lb[L, 0:128]
        cols[:, 8 * L + 1] = hgrn_lb[L, 128:256]
        cols[:, 8 * L + 2] = np.tile(hgrn_onorm_g[L], 2)
    o["cols"] = cols
    o["norm_g"] = np.ascontiguousarray(norm_g.astype(f))
    o["fin_g"] = np.ascontiguousarray(final_norm_g.reshape(1, 1024).astype(f))
    o["ln_g"] = np.ascontiguousarray(gmlp_ln_g.reshape(DEPTH * 2, 128).astype(f))
    o["ln_b"] = np.ascontiguousarray(gmlp_ln_b.reshape(DEPTH * 2, 128).astype(f))
    o["ws_t"] = np.ascontiguousarray(np.transpose(gmlp_w_s, (0, 1, 3, 2)).reshape(DEPTH * 2, 2, 128, 128).astype(f))
    o["b_s"] = np.ascontiguousarray(gmlp_b_s.reshape(DEPTH * 4, 128).astype(f))
    o["b_f"] = np.ascontiguousarray(fox_b_f.reshape(DEPTH * 2, 4).astype(f))
    return o


_CACHE = {}


def kernel(x, norm_g, w_in, w_out, gmlp_ln_g, gmlp_ln_b, gmlp_w_s, gmlp_b_s,
           hgrn_lb, hgrn_onorm_g, fox_b_f, final_norm_g):
    x = np.asarray(x, np.float32)
    B, SEQ, D = x.shape
    DEPTH = w_in.shape[0]
    shared = _prep_shared(*[np.asarray(a, np.float32) for a in (norm_g, w_in, w_out, gmlp_ln_g, gmlp_ln_b, gmlp_w_s,
                                                                 gmlp_b_s, hgrn_lb, hgrn_onorm_g, fox_b_f, final_norm_g)])
    nc = build_program(SEQ, DEPTH)
    work = [0, 1, 4, 5]
    idle = {k: np.zeros_like(v) for k, v in shared.items()}
    idle["consts"] = shared["consts"]
    xz = np.zeros((SEQ, D), np.float32)
    in_maps = []
    for core in range(8):
        if core in work:
            m = dict(shared)
            m["x"] = np.ascontiguousarray(x[work.index(core)])
        else:
            m = dict(idle)
            m["x"] = xz
        in_maps.append(m)
    res = run_bass_kernel_spmd(nc, in_maps, core_ids=list(range(8)))
    return np.stack([res.results[work[b]]["out"] for b in range(B)], axis=0).astype(np.float32)
```

```python
from contextlib import ExitStack
from dataclasses import dataclass

import numpy as np
import concourse.bass as bass
import concourse.mybir as mybir
from concourse.bass_utils import run_bass_kernel_spmd

F32 = mybir.dt.float32
BF16 = mybir.dt.bfloat16
AF = mybir.ActivationFunctionType
ALU = mybir.AluOpType

SAME_ENGINE_SYNC = True
EPOCH = 20000
NSLOT = 8


@dataclass(frozen=True)
class Prod:
    eng: str
    is_dma: bool
    semkey: tuple
    val: int


class Buf:
    def __init__(self, name, ap, root=None):
        self.name = name
        self.ap = ap
        self.w = []
        self.r = []
        self.root = root.root if root is not None else self


class Prog:
    ENGS = ("pe", "act", "dve", "pool", "sp")

    def __init__(self, nc):
        self.nc = nc
        self.stack = ExitStack()
        self.ops = {e: [] for e in self.ENGS}
        self.ncomp = {e: 0 for e in self.ENGS}
        self.ndma = {e: 0 for e in self.ENGS}
        self.known = {e: {} for e in self.ENGS}
        self.semkeys = []
        self.finals = []
        self.slot_last = {}
        self.tag = "setup"
        self.namemap = {}

    def sb(self, name, shape, dtype):
        h = self.stack.enter_context(self.nc.sbuf_tensor("sb_" + name, list(shape), dtype))
        return Buf(name, h[:])

    def ps(self, name, shape, dtype=F32):
        h = self.stack.enter_context(self.nc.psum_tensor(name, list(shape), dtype))
        return Buf(name, h[:])

    def _deps(self, eng, reads, writes, me_dma=False):
        raw = []
        for b in reads:
            raw.extend(b.root.w)
        wxx = []
        for b in writes:
            wxx.extend(b.root.w)
            wxx.extend(b.root.r)
        need = {}
        for d in raw:
            if (not d.is_dma) and d.eng == eng and not me_dma and (eng == "pe" or not SAME_ENGINE_SYNC):
                continue
            if need.get(d.semkey, 0) < d.val:
                need[d.semkey] = d.val
        for d in wxx:
            if (not d.is_dma) and d.eng == eng and not me_dma:
                continue
            if need.get(d.semkey, 0) < d.val:
                need[d.semkey] = d.val
        waits = []
        kn = self.known[eng]
        for sk, v in need.items():
            if kn.get(sk, 0) >= v:
                continue
            kn[sk] = v
            waits.append((sk, v))
        return waits

    def _commit(self, me, reads, writes, pwrites):
        for b in reads:
            b.root.r.append(me)
        for b in writes:
            b.root.w = [me]
            b.root.r = []
        for b in pwrites:
            b.root.w = [w for w in b.root.w if w.is_dma or w.eng != me.eng] + [me]
            b.root.r = []

    def capture(self):
        self._cap = []

    def end_capture(self):
        lst, self._cap = self._cap, None
        return lst

    @staticmethod
    def merge_lists(a, b):
        out = []
        na, nb = len(a), len(b)
        ia = ib = 0
        while ia < na or ib < nb:
            if ib >= nb or (ia < na and ia * nb <= ib * na):
                out.append(a[ia]); ia += 1
            else:
                out.append(b[ib]); ib += 1
        return out

    def replay_merged(self, a, b):
        na, nb = len(a), len(b)
        ia = ib = 0
        while ia < na or ib < nb:
            take_a = ib >= nb or (ia < na and ia * nb <= ib * na)
            kind, args, tag = a[ia] if take_a else b[ib]
            if take_a:
                ia += 1
            else:
                ib += 1
            self.tag = tag
            (self.op if kind == "op" else self.dma)(*args)

    def op(self, eng, fn, reads=(), writes=(), pwrites=()):
        if getattr(self, "_cap", None) is not None:
            self._cap.append(("op", (eng, fn, reads, writes, pwrites), self.tag))
            return None
        waits = self._deps(eng, reads, list(writes) + list(pwrites))
        n = self.ncomp[eng]
        self.ncomp[eng] = n + 1
        semkey = ("c", eng, n // EPOCH)
        if semkey not in self.semkeys:
            self.semkeys.append(semkey)
        me = Prod(eng, False, semkey, n % EPOCH + 1)
        self.ops[eng].append((fn, waits, semkey, 1, self.tag))
        self._commit(me, reads, writes, pwrites)
        return me

    def dma(self, eng, fn, reads=(), writes=(), pwrites=(), final=False):
        if getattr(self, "_cap", None) is not None:
            self._cap.append(("dma", (eng, fn, reads, writes, pwrites, final), self.tag))
            return None
        waits = self._deps(eng, reads, list(writes) + list(pwrites), me_dma=True)
        n = self.ndma[eng]
        self.ndma[eng] = n + 1
        slot = n % NSLOT
        semkey = ("d", eng, slot)
        if semkey not in self.semkeys:
            self.semkeys.append(semkey)
        val = 16 * (n // NSLOT + 1)
        kn = self.known[eng]
        if val > 16 and kn.get(semkey, 0) < val - 16:
            kn[semkey] = val - 16
            waits.append((semkey, val - 16))
        me = Prod(eng, True, semkey, val)
        self.ops[eng].append((fn, waits, semkey, 16, self.tag))
        self._commit(me, reads, writes, pwrites)
        if final:
            self.finals.append(me)
        return me

    def emit(self):
        nc = self.nc
        with ExitStack() as st:
            sems = {}
            for k in self.semkeys:
                sems[k] = st.enter_context(nc.semaphore("s_" + "_".join(str(x) for x in k)))

            def replay(eng, e):
                for fn, waits, semkey, inc, tag in self.ops[eng]:
                    for sk, v in waits:
                        e.wait_ge(sems[sk], v)
                    ins = fn(e)
                    ins.then_inc(sems[semkey], inc)
                    try:
                        self.namemap[ins.ins.name] = tag
                    except Exception:
                        pass
                if eng == "sp":
                    done = {}
                    for f in self.finals:
                        done[f.semkey] = max(done.get(f.semkey, 0), f.val)
                    for sk, v in done.items():
                        e.wait_ge(sems[sk], v)

            with nc.Block() as block:
                @block.sync
                def _(e):
                    replay("sp", e)

                @block.scalar
                def _(e):
                    replay("act", e)

                @block.vector
                def _(e):
                    replay("dve", e)

                @block.gpsimd
                def _(e):
                    replay("pool", e)

                @block.tensor
                def _(e):
                    replay("pe", e)
        self.stack.close()


def _consts():
    p = np.arange(128)
    ident = np.eye(128, dtype=np.float32)
    U = (p[:, None] <= p[None, :]).astype(np.float32)
    perm = (p[:, None] == ((p[None, :] + 64) % 128)).astype(np.float32)
    blk = ((p[:, None] // 64) == (p[None, :] // 64)).astype(np.float32)
    rmask = np.ones((128, 512), np.float32)
    rmask[:, ::128] = 0.0
    ubig = np.concatenate([U, np.ones((128, 384), np.float32)], axis=1)
    ones = np.ones((128, 128), np.float32)
    negm = ((U - 1.0) * 30000.0).astype(np.float32)
    return np.ascontiguousarray(np.concatenate([ident, U, perm, blk, rmask, ubig, ones, negm], axis=1))


C_ID, C_U, C_PERM, C_BLK, C_RM, C_UB, C_ONE, C_NEG = 0, 128, 256, 384, 512, 1024, 1536, 1664
NCONST = 1792
FM_COLS = 1408
TM1_C0 = 1408
TM2_C0 = 1920
WCOLS = 1924
EPS = 1e-6


def cap(ap, dims):
    return bass.AP(ap.tensor, ap.offset, [list(ap.ap[0])] + [list(d) for d in dims])


class _Stop(Exception):
    pass


def build_program(SEQ, DEPTH, dbg=False, stop=0):
    nc = bass.Bass("TRN2", target_bir_lowering=False)
    NSB = SEQ // 512
    NCH = SEQ // 128
    D = 1024

    def din(name, shape, dt=F32):
        return nc.dram_tensor(name, list(shape), dt, kind="ExternalInput").ap()

    x_d = din("x", [SEQ, D])
    win_d = din("w_in", [DEPTH * 2 * D, WCOLS])
    wout_d = din("w_out", [DEPTH * D, D])
    cst_d = din("consts", [128, NCONST])
    cols_d = din("cols", [128, 8 * DEPTH])
    ng_d = din("norm_g", [DEPTH, D])
    fg_d = din("fin_g", [1, D])
    lng_d = din("ln_g", [DEPTH * 2, 128])
    lnb_d = din("ln_b", [DEPTH * 2, 128])
    wst_d = din("ws_t", [DEPTH * 2, 2, 128, 128])
    bs_d = din("b_s", [DEPTH * 4, 128])
    bf_d = din("b_f", [DEPTH * 2, 4])
    out_d = nc.dram_tensor("out", [SEQ, D], F32, kind="ExternalOutput").ap()
    x1_d = nc.dram_tensor("x1_scr", [SEQ, D], F32, kind="Internal").ap()
    ht_d = nc.dram_tensor("ht_scr", [NSB, 128, 8 * 512], BF16, kind="Internal").ap()
    yh_d = nc.dram_tensor("yh_scr", [NSB, 128, 4 * 512], BF16, kind="Internal").ap()
    if dbg:
        dbg_d = nc.dram_tensor("dbg", [DEPTH * 2 * NSB, 128, 4 * 512], F32, kind="ExternalOutput").ap()

    P = Prog(nc)
    sb, ps = P.sb, P.ps

    cst = sb("cst", [128, NCONST], F32)
    cols = sb("cols", [128, 8 * DEPTH], F32)
    identb = sb("identb", [128, 128], BF16)
    win = sb("win", [128, 8, WCOLS], BF16)
    WG = [(g * 256, min((g + 1) * 256, WCOLS)) for g in range(8)]
    winG = [Buf(f"win_g{g}", win.ap[:, :, c0:c1]) for g, (c0, c1) in enumerate(WG)]
    wout = sb("wout", [128, 8, D], BF16)
    KTa = [sb(f"KTa{h}", [128, SEQ], BF16) for h in range(4)]
    Vst = sb("Vst", [128, NCH, 384], BF16)
    g_bc = sb("g_bc", [128, D], F32)
    gf_bc = g_bc
    lng_bc = sb("lng_bc", [128, 128], F32)
    lnb_bc = sb("lnb_bc", [128, 128], F32)
    wtf = sb("wtf", [128, 2, 128], F32)
    WTm = sb("WTm", [128, 2, 128], BF16)
    bs_bc = sb("bs_bc", [128, 128], F32)
    bf_bc = sb("bf_bc", [128, 4], F32)
    lbc = sb("lbc", [128, 2], F32)
    epsc = sb("epsc", [128, 1], F32)
    xt = [sb(f"xt{i}", [128, D], F32) for i in range(2)]
    ssq = sb("ssq", [128, 2], F32)
    ssqL = [sb(f"ssqL{i}", [128, 2], F32) for i in range(2)]
    hT = sb("hT", [128, 8, 512], BF16)
    yT = sb("yT", [128, 4, 512], BF16)
    yO = sb("yO", [128, 4, 512], BF16)
    ua = sb("ua", [128, 512], F32)
    sza = sb("sza", [128, 512], F32)
    atmp = sza
    vaL = [sb(f"va{i}", [128, 128], F32) for i in range(4)]
    vsqL = [sb(f"vsq{i}", [128, 128], F32) for i in range(4)]
    lstL = [sb(f"lst{i}", [128, 8], F32) for i in range(4)]
    vnL = [sb(f"vn{i}", [128, 128], BF16) for i in range(4)]
    sq = sb("sq", [128, 512], F32)
    fT = sb("fT", [128, 512], F32)
    lf = sb("lf", [128, 512], F32)
    kf = sb("kf", [128, 512], F32)
    bT = sb("bT", [128, 512], F32)
    dd = [sb(f"dd{i}", [128, 512], F32) for i in range(3)]
    E1 = lf
    Et = [dd[2], fT, dd[0]]
    Etc = [sb(f"Etc{i}", [128, 4, 64], F32) for i in range(2)]
    Q1 = sb("Q1", [128, 512], BF16)
    K1 = sb("K1", [128, 512], BF16)
    QD = sb("QD", [128, 512], BF16)
    KD = sb("KD", [128, 512], BF16)
    QC = sb("QC", [128, 4, 64], BF16)
    KC = sb("KC", [128, 4, 64], BF16)
    K1tok = sb("K1tok", [128, 4, 128], BF16)
    Vb = sb("Vb", [128, 4, 128], BF16)
    scT = [sb(f"scT{i}", [128, 2, 128], BF16) for i in range(4)]
    Sf = sb("Sf", [128, 64], F32)
    Spad = [sb(f"Spad{i}", [128, 5, 64], BF16) for i in range(2)]
    szb = sb("szb", [128, 512], F32)
    osb, osq, rtb = sq, kf, bT
    Qa = [sb(f"Qa{h}", [128, 512], BF16) for h in range(4)]
    szc = [sb(f"szc{i}", [128, 512], F32) for i in range(2)]
    xf = sb("xf", [128, 16], F32)
    nl = sb("nl", [128, 4, 4], F32)
    Cabs = sb("Cabs", [128, NCH, 4], F32)
    carry = sb("carry", [128, NCH + 1, 4], F32)
    biasQ = sb("biasQ", [128, 4, NCH], F32)
    Lp1 = sb("Lp", [128, 4, 128], F32)
    Lp = [Lp1, Lp1]
    pT = [sb(f"pT{i}", [128, 512], BF16) for i in range(4)]
    X = sb("X", [128, 2, 512], F32)
    rl, t1 = dd[1], dd[2]
    junk = Buf("junk", X.ap.rearrange("p a t -> p (a t)"), root=X)
    xr = xt
    xn = [sb(f"xn{i}", [128, D], F32) for i in range(2)]
    hn = [Buf(f"hn{i}", xn[i].ap.bitcast(BF16)[:, 0:D], root=xn[i]) for i in range(2)]
    xo = xn
    Sb = [ps(f"psS{i}", [128, 512]) for i in range(2)]
    Ob = [ps(f"psO{i}", [128, 512]) for i in range(2)]
    Gb = [ps(f"psG{i}", [128, 512]) for i in range(2)]
    Tb = [ps(f"psT{i}", [128, 1024], BF16) for i in range(2)]
    Tf = Buf("psTf", Tb[1].ap.bitcast(F32), root=Tb[1])
    Tf0 = Buf("psTf0", Tb[0].ap.bitcast(F32), root=Tb[0])
    gi = [0]
    ti = [0]

    def G():
        gi[0] += 1
        return Gb[gi[0] % 2]

    def T():
        ti[0] += 1
        return Tb[ti[0] % 2]

    x1_t = [Buf(f"x1_{i}", None) for i in range(NCH)]
    ht_t = [Buf(f"ht_{i}", None) for i in range(NSB)]
    yh_t = [Buf(f"yh_{i}", None) for i in range(NSB)]

    def C(off, n=128):
        return cst.ap[:, off:off + n]

    def mm(out_ap, pairs, reads, writes=(), pwrites=()):
        def fn(e, out_ap=out_ap, pairs=pairs):
            n = len(pairs)
            ins = None
            for i, (l, r) in enumerate(pairs):
                ins = e.matmul(out_ap, lhsT=l, rhs=r, start=(i == 0), stop=(i == n - 1))
            return ins
        P.op("pe", fn, reads, writes, pwrites)

    def act(out, in_, func, reads, writes=(), pwrites=(), **kw):
        P.op("act", lambda e: e.activation(out=out, in_=in_, func=func, **kw), reads, writes, pwrites)

    def tt(eng, out, in0, in1, op, reads, writes=(), pwrites=()):
        P.op(eng, lambda e: e.tensor_tensor(out=out, in0=in0, in1=in1, op=op), reads, writes, pwrites)

    def ts(eng, out, in0, s1, s2, op0, op1, reads, writes=(), pwrites=()):
        if s2 is None:
            P.op(eng, lambda e: e.tensor_scalar(out=out, in0=in0, scalar1=s1, scalar2=None, op0=op0), reads, writes, pwrites)
        else:
            P.op(eng, lambda e: e.tensor_scalar(out=out, in0=in0, scalar1=s1, scalar2=s2, op0=op0, op1=op1), reads, writes, pwrites)

    def stt(eng, out, in0, scalar, in1, op0, op1, reads, writes=(), pwrites=()):
        P.op(eng, lambda e: e.scalar_tensor_tensor(out=out, in0=in0, scalar=scalar, in1=in1, op0=op0, op1=op1), reads, writes, pwrites)

    def cp(eng, out, in_, reads, writes=(), pwrites=()):
        if eng == "act":
            P.op("act", lambda e: e.copy(out=out, in_=in_), reads, writes, pwrites)
        else:
            P.op(eng, lambda e: e.tensor_copy(out=out, in_=in_), reads, writes, pwrites)

    def mset(eng, buf, val, ap=None):
        a = buf.ap if ap is None else ap
        P.op(eng, lambda e: e.memset(a, val), [], [buf] if ap is None else [], [] if ap is None else [buf])

    def bcast_rows(src_ap_row, nparts, n):
        return bass.AP(src_ap_row.tensor, src_ap_row.offset, [[0, nparts], [1, n]])

    P.dma("sp", lambda e: e.dma_start(out=cst.ap, in_=cst_d), [], [cst])
    P.dma("sp", lambda e: e.dma_start(out=cols.ap, in_=cols_d), [], [cols])
    cp("dve", identb.ap, C(C_ID), [cst], [identb])
    mset("dve", epsc, EPS)
    mset("pool", Vst, 1.0)
    for i in range(4):
        mset("pool", scT[i], 0.0)
    mset("pool", Lp1, 0.0)
    for h in range(4):
        mset("pool", KTa[h], 1.0)
        mset("pool", Qa[h], 0.0)

    def load_weights(L, p, do_win=True, do_wout=True):
        r0 = (L * 2 + p) * D
        for g in range(8 if do_win else 0):
            c0, c1 = WG[g]
            P.dma("pool", lambda e, g=g, c0=c0, c1=c1: e.dma_start(
                out=winG[g].ap, in_=win_d[r0:r0 + D, c0:c1].rearrange("(a p) n -> p a n", p=128)), [], [winG[g]])
        if p == 0 and do_wout:
            for et in range(8):
                P.dma("pool", lambda e, et=et: e.dma_start(out=wout.ap[:, et, :], in_=wout_d[L * D + et * 128: L * D + (et + 1) * 128, :]),
                      [], [], [wout])

    def setup_pass(L, p):
        lp = L * 2 + p
        if p == 0:
            P.dma("sp", lambda e: e.dma_start(out=g_bc.ap, in_=bcast_rows(ng_d[L:L + 1, :], 128, D)), [], [g_bc])
        elif L == DEPTH - 1:
            P.dma("sp", lambda e: e.dma_start(out=g_bc.ap, in_=bcast_rows(fg_d[0:1, :], 128, D)), [], [g_bc])
        P.dma("sp", lambda e: e.dma_start(out=lng_bc.ap, in_=bcast_rows(lng_d[lp:lp + 1, :], 128, 128)), [], [lng_bc])
        P.dma("sp", lambda e: e.dma_start(out=lnb_bc.ap, in_=bcast_rows(lnb_d[lp:lp + 1, :], 128, 128)), [], [lnb_bc])
        P.dma("sp", lambda e: e.dma_start(out=bf_bc.ap, in_=bcast_rows(bf_d[lp:lp + 1, :], 128, 4)), [], [bf_bc])
        P.dma("sp", lambda e: e.dma_start(out=wtf.ap, in_=wst_d[lp].rearrange("g s t -> s g t")), [], [wtf])
        for g2 in range(2):
            P.dma("sp", lambda e, g2=g2: e.dma_start(out=bs_bc.ap[g2 * 64:(g2 + 1) * 64, :],
                                                    in_=bcast_rows(bs_d[lp * 2 + g2: lp * 2 + g2 + 1, :], 64, 128)),
                  [], [], [bs_bc])
        tt("dve", WTm.ap, wtf.ap, cap(C(C_U), [[0, 2], [1, 128]]), ALU.mult, [wtf, cst], [WTm])
        if L == 0:
            mset("dve", lbc, 0.0, lbc.ap[:, 0:1])
            mset("dve", lbc, 1.0, lbc.ap[:, 1:2])
        else:
            c0 = cols.ap[:, p:p + 1]
            c1 = cols.ap[:, 8 + p:8 + p + 1]
            tt("dve", lbc.ap[:, 0:1], c1, c0, ALU.subtract, [cols], [lbc])
            act(lbc.ap[:, 0:1], lbc.ap[:, 0:1], AF.Sigmoid, [lbc], [lbc])
            ts("dve", lbc.ap[:, 0:1], lbc.ap[:, 0:1], 1.0 - 1e-6, None, ALU.min, None, [lbc], [lbc])
            ts("dve", lbc.ap[:, 1:2], lbc.ap[:, 0:1], -1.0, 1.0, ALU.mult, ALU.add, [lbc], [lbc])
        mset("dve", Sf, 0.0)
        mset("dve", Spad[0], 0.0)
        mset("dve", Spad[1], 0.0)
        mset("dve", carry, 0.0, carry.ap[:, 0, :])

    ETILE = [[0, 2, 4, 5], [1, 3, 6, 7]]
    pend = [[]]

    def main_loops():
      for L in range(DEPTH):
          xsrc = x_d if L == 0 else x1_d
          last = (L == DEPTH - 1)
          for p in range(2):
              load_weights(L, p, do_win=(L == 0 and p == 0))
              setup_pass(L, p)
              ong = cols.ap[:, 8 * L + 2: 8 * L + 3]
              for sbi in range(NSB):
                  t0 = sbi * 512
                  P.tag = "1hT"
                  def h_phase(sbt):
                      def h_stats(c):
                          ci = sbt * 4 + c
                          xb, hb, sq_ = xt[c % 2], hn[c % 2], ssqL[c % 2]
                          rd = [x1_t[ci]] if L > 0 else []
                          P.dma("sp", lambda e, xb=xb, ci=ci, xsrc=xsrc: e.dma_start(out=xb.ap, in_=xsrc[ci * 128:(ci + 1) * 128, :]), rd, [xb])
                          sc = sq_.ap[:, 0:1]
                          act(junk.ap, xb.ap, AF.Square, [xb], [X, sq_], accum_out=sc)
                          act(sq_.ap[:, 1:2], sc, AF.Sqrt, [sq_, epsc], [sq_], scale=1.0 / D, bias=epsc.ap[:, 0:1])
                          P.op("dve", lambda e, sq_=sq_: e.reciprocal(out=sq_.ap[:, 1:2], in_=sq_.ap[:, 1:2]), [sq_], [sq_])
                          stt("dve", hb.ap, xb.ap, sq_.ap[:, 1:2], g_bc.ap, ALU.mult, ALU.mult, [xb, sq_, g_bc], [hb])

                      def h_xpose(c):
                          hb = hn[c % 2]
                          tb_ = T()
                          for dt in range(8):
                              P.op("pe", lambda e, tb_=tb_, dt=dt, hb=hb: e.transpose(
                                  out=tb_.ap[:, dt * 128:(dt + 1) * 128], in_=hb.ap[:, dt * 128:(dt + 1) * 128], identity=identb.ap),
                                  [hb, identb], [], [tb_])
                          cp("act" if c % 2 == 0 else "dve", hT.ap[:, :, c * 128:(c + 1) * 128],
                             tb_.ap.rearrange("p (k t) -> p k t", k=8), [tb_], [], [hT])

                      h_stats(0)
                      for c in range(4):
                          if c + 1 < 4:
                              h_stats(c + 1)
                          h_xpose(c)
                      P.dma("sp", lambda e, sbt=sbt: e.dma_start(out=ht_d[sbt], in_=hT.ap.rearrange("p a t -> p (a t)")), [hT], [ht_t[sbt]])

                  def h_load(sbt):
                      P.dma("sp", lambda e, sbt=sbt: e.dma_start(out=hT.ap.rearrange("p a t -> p (a t)"), in_=ht_d[sbt]), [ht_t[sbt]], [hT])

                  if sbi == 0 and p == 0:
                      h_phase(0)

                  if stop == 1:
                      raise _Stop()
                  P.tag = "2FM"
                  P.capture()
                  def fm(ft):
                      g = G()
                      mm(g.ap, [(win.ap[:, dt, ft * 128:(ft + 1) * 128], hT.ap[:, dt, :]) for dt in range(8)], [winG[ft // 2], hT], [g])
                      return g

                  g = fm(0); act(ua.ap, g.ap, AF.Gelu_apprx_tanh, [g], [ua])
                  g = fm(1); act(sza.ap, g.ap, AF.Silu, [g], [sza])
                  tt("pool", ua.ap, ua.ap, sza.ap, ALU.mult, [ua, sza], [ua])
                  g = fm(2); act(sq.ap, g.ap, AF.Silu, [g], [sq])
                  g = fm(3); act(fT.ap, g.ap, AF.Sigmoid, [g], [fT])
                  g = fm(4); act(szb.ap, g.ap, AF.Silu, [g], [szb])
                  psq = [None, None]
                  for pr in range(2):
                      g = fm(5 + pr)
                      P.op("act", lambda e, g=g, pr=pr: e.mul(out=Qa[2 * pr].ap[0:64, :], in_=g.ap[0:64, :], mul=0.125), [g], [], [Qa[2 * pr]])
                      ts("dve", Qa[2 * pr + 1].ap[64:128, :], g.ap[64:128, :], 0.125, None, ALU.mult, None, [g], [], [Qa[2 * pr + 1]])
                  for pr in range(2):
                      g = fm(7 + pr)
                      cp("act", KTa[2 * pr].ap[0:64, t0:t0 + 512], g.ap[0:64, :], [g], [], [KTa[2 * pr]])
                      cp("dve", KTa[2 * pr + 1].ap[64:128, t0:t0 + 512], g.ap[64:128, :], [g], [], [KTa[2 * pr + 1]])
                  for pr in range(2):
                      g = fm(9 + pr); act(szc[pr].ap, g.ap, AF.Silu, [g], [szc[pr]])

                  listFM = P.end_capture()
                  P.tag = "3TMA"
                  P.capture()
                  psA = Ob[0]
                  psF = Tf
                  for c in range(4):
                      ci = sbi * 4 + c
                      va = vaL[c]
                      g = [Sb[0], Sb[1], Tf0, Sb[0]][c]
                      mm(g.ap, [(hT.ap[:, dt, c * 128:(c + 1) * 128], win.ap[:, dt, TM1_C0:TM1_C0 + 512]) for dt in range(8)], [winG[5], winG[6], winG[7], hT], [g])
                      act(va.ap, g.ap[:, 0:128], AF.Gelu_apprx_tanh, [g], [va])
                      cp("dve", Vb.ap[:, c, :], g.ap[:, 128:256], [g, va], [], [Vb])
                      for pr in range(2):
                          cp("dve",
                             cap(Vst.ap[:, ci, pr * 192:pr * 192 + 1], [[128, 2], [1, 64]]),
                             g.ap[:, 256 + pr * 128: 256 + (pr + 1) * 128].rearrange("p (h d) -> p h d", h=2), [g, va], [], [Vst])
                      mm(psF.ap[:, c * 4:(c + 1) * 4],
                         [(hT.ap[:, dt, c * 128:(c + 1) * 128], win.ap[:, dt, TM2_C0:TM2_C0 + 4]) for dt in range(8)], [winG[7], hT], [], [psF])

                  listTM = P.end_capture()
                  P.capture()

                  def ln_stage(s, c):
                      va, vsq, lst, vn = vaL[c], vsqL[c], lstL[c], vnL[c]
                      vc = vsq
                      va3 = va.ap.rearrange("p (g d) -> p g d", g=2)
                      vc3 = vc.ap.rearrange("p (g d) -> p g d", g=2)
                      if s == 0:
                          P.op("dve", lambda e, va3=va3, lst=lst: e.tensor_reduce(out=lst.ap[:, 0:2], in_=va3, axis=mybir.AxisListType.X, op=ALU.add),
                               [va], [], [lst])
                          tt("pool", vsq.ap, va.ap, va.ap, ALU.mult, [va], [vsq])
                      elif s == 1:
                          P.op("dve", lambda e, vsq=vsq, lst=lst: e.tensor_reduce(out=lst.ap[:, 2:4], in_=vsq.ap.rearrange("p (g d) -> p g d", g=2),
                                                                                  axis=mybir.AxisListType.X, op=ALU.add), [vsq], [], [lst])
                          ts("dve", lst.ap[:, 4:6], lst.ap[:, 0:2], 1.0 / 64, None, ALU.mult, None, [lst], [], [lst])
                      elif s == 2:
                          tt("dve", lst.ap[:, 0:2], lst.ap[:, 4:6], lst.ap[:, 4:6], ALU.mult, [lst], [], [lst])
                      elif s == 3:
                          stt("dve", lst.ap[:, 6:8], lst.ap[:, 2:4], 1.0 / 64, lst.ap[:, 0:2], ALU.mult, ALU.subtract, [lst], [], [lst])
                      elif s == 4:
                          act(lst.ap[:, 6:8], lst.ap[:, 6:8], AF.Sqrt, [lst, epsc], [], [lst], bias=epsc.ap[:, 0:1])
                      elif s == 5:
                          P.op("dve", lambda e, lst=lst: e.reciprocal(out=lst.ap[:, 6:8], in_=lst.ap[:, 6:8]), [lst], [], [lst])
                      elif s == 6:
                          tt("dve", vc3, va3, cap(lst.ap[:, 4:5], [[1, 2], [0, 64]]), ALU.subtract, [va, lst], [vc])
                      elif s == 7:
                          tt("dve", vc3, vc3, cap(lst.ap[:, 6:7], [[1, 2], [0, 64]]), ALU.mult, [vc, lst], [vc])
                      elif s == 8:
                          tt("pool", vc.ap, vc.ap, lng_bc.ap, ALU.mult, [vc, lng_bc], [vc])
                      elif s == 9:
                          tt("pool", vn.ap, vc.ap, lnb_bc.ap, ALU.add, [vc, lnb_bc], [vn])
                      elif s == 10:
                          for g2 in range(2):
                              mm(psA.ap[g2 * 64:(g2 + 1) * 64, c * 128:(c + 1) * 128],
                                 [(vn.ap[:, g2 * 64:(g2 + 1) * 64], WTm.ap[:, g2, :])], [vn, WTm], [], [psA])

                  for s in range(11):
                      for c in range(4):
                          ln_stage(s, c)
                  tt("dve", atmp.ap.rearrange("p (c t) -> p c t", c=4), psA.ap.rearrange("p (c t) -> p c t", c=4),
                     cap(bs_bc.ap[:, 0:1], [[0, 4], [1, 128]]), ALU.add, [psA, bs_bc], [atmp])
                  tt("pool", yT.ap[:, 0, :], atmp.ap, ua.ap, ALU.mult, [atmp, ua], [], [yT])

                  list3 = P.end_capture()
                  P.tag = "5aCset"
                  P.capture()
                  tt("dve", xf.ap.rearrange("p (c h) -> p c h", c=4), psF.ap[:, 0:16].rearrange("p (c h) -> p c h", c=4),
                     cap(bf_bc.ap[:, 0:1], [[0, 4], [1, 4]]), ALU.add, [psF, bf_bc], [xf])
                  act(xf.ap, xf.ap, AF.Exp, [xf], [xf], scale=-1.0)
                  act(nl.ap.rearrange("p c h -> p (c h)"), xf.ap, AF.Ln, [xf], [nl], bias=1.0)
                  gC = Sb[0]
                  mm(gC.ap[:, 0:16], [(C(C_U), nl.ap.rearrange("p c h -> p (c h)"))], [cst, nl], [gC])
                  mm(gC.ap[:, 16:32], [(C(C_ONE), nl.ap.rearrange("p c h -> p (c h)"))], [cst, nl], [], [gC])
                  for c in range(4):
                      ci = sbi * 4 + c
                      tt("dve", Cabs.ap[:, ci, :], gC.ap[:, c * 4:(c + 1) * 4], carry.ap[:, ci, :], ALU.add, [gC, carry], [], [Cabs])
                      tt("dve", carry.ap[:, ci + 1, :], gC.ap[:, 16 + c * 4:16 + (c + 1) * 4], carry.ap[:, ci, :], ALU.add, [gC, carry], [], [carry])
                  nkb = 4 * sbi + 4
                  for hd in range(4):
                      ts("dve", biasQ.ap[:, hd, 0:nkb], cap(Cabs.ap[:, 0, hd:hd + 1], [[4, nkb]]), carry.ap[:, 4 * sbi, hd:hd + 1], None,
                         ALU.subtract, None, [Cabs, carry], [], [biasQ])
                  for pr in range(2):
                      hA, hB = 2 * pr, 2 * pr + 1
                      cp("pool", Lp[pr].ap[:, :, 64:65], nl.ap[:, :, hA:hA + 1], [nl], [], [Lp[pr]])
                      cp("pool", Lp[pr].ap[:, :, 0:1], nl.ap[:, :, hB:hB + 1], [nl], [], [Lp[pr]])
                      gR = Sb[1] if pr == 0 else Gb[0]
                      def fnr(e, gR=gR, pr=pr):
                          ins = None
                          for c in range(4):
                              ins = e.matmul(gR.ap[:, c * 128:512], lhsT=Lp[pr].ap[:, c, :], rhs=C(C_UB, 512 - c * 128),
                                             start=(c == 0), stop=(c == 3))
                          return ins
                      P.op("pe", fnr, [Lp[pr], cst], [gR])
                      ts("dve", Qa[hA].ap[64:128, :], gR.ap[64:128, :], -1.0, None, ALU.mult, None, [gR], [], [Qa[hA]])
                      ts("dve", Qa[hB].ap[0:64, :], gR.ap[0:64, :], -1.0, None, ALU.mult, None, [gR], [], [Qa[hB]])
                  listCs = P.end_capture()
                  P.tag = "4B"
                  P.capture()
                  ts("dve", fT.ap, fT.ap, lbc.ap[:, 1:2], lbc.ap[:, 0:1], ALU.mult, ALU.add, [fT, lbc], [fT])
                  ts("dve", fT.ap, fT.ap, 1e-30, None, ALU.max, None, [fT], [fT])
                  act(lf.ap, fT.ap, AF.Ln, [fT], [lf])
                  ts("pool", kf.ap, fT.ap, -1.0, 1.0, ALU.mult, ALU.add, [fT], [kf])
                  P.op("dve", lambda e: e.tensor_tensor_scan(out=bT.ap, data0=C(C_RM, 512), data1=lf.ap, initial=0.0,
                                                             op0=ALU.mult, op1=ALU.add), [lf, cst], [bT])
                  b8 = bT.ap.rearrange("p (a t) -> p a t", a=8)
                  b4 = bT.ap.rearrange("p (a t) -> p a t", a=4)
                  tt("dve", dd[0].ap.rearrange("p (a t) -> p a t", a=8), b8, cap(bT.ap[:, 31:32], [[64, 8], [0, 64]]), ALU.subtract, [bT], [dd[0]])
                  tt("dve", dd[1].ap.rearrange("p (a t) -> p a t", a=4), b4, cap(bT.ap[:, 63:64], [[128, 4], [0, 128]]), ALU.subtract, [bT], [dd[1]])
                  tt("pool", dd[2].ap.rearrange("p (a t) -> p a t", a=4), b4, cap(bT.ap[:, 127:128], [[128, 4], [0, 128]]), ALU.subtract, [bT], [dd[2]])
                  act(E1.ap, bT.ap, AF.Exp, [bT], [E1])
                  act(Et[0].ap, dd[2].ap, AF.Exp, [dd[2]], [Et[0]], scale=-1.0)
                  act(Et[1].ap, dd[0].ap, AF.Exp, [dd[0]], [Et[1]])
                  act(Et[2].ap, dd[0].ap, AF.Exp, [dd[0]], [Et[2]], scale=-1.0)
                  d14 = dd[1].ap.rearrange("p (a t) -> p a t", a=4)
                  act(Etc[0].ap, d14[:, :, 64:128], AF.Exp, [dd[1]], [Etc[0]])
                  act(Etc[1].ap, d14[:, :, 0:64], AF.Exp, [dd[1]], [Etc[1]], scale=-1.0)
                  sq4 = sq.ap.rearrange("p (a t) -> p a t", a=4)
                  kf4 = kf.ap.rearrange("p (a t) -> p a t", a=4)
                  tt("dve", Q1.ap, sq.ap, E1.ap, ALU.mult, [sq, E1], [Q1])
                  tt("pool", K1.ap, kf.ap, Et[0].ap, ALU.mult, [kf, Et[0]], [K1])
                  tt("dve", QD.ap, sq.ap, Et[1].ap, ALU.mult, [sq, Et[1]], [QD])
                  tt("pool", KD.ap, kf.ap, Et[2].ap, ALU.mult, [kf, Et[2]], [KD])
                  tt("dve", QC.ap, sq4[:, :, 64:128], Etc[0].ap, ALU.mult, [sq, Etc[0]], [QC])
                  tt("pool", KC.ap, kf4[:, :, 0:64], Etc[1].ap, ALU.mult, [kf, Etc[1]], [KC])
                  tb_ = Tb[1]
                  for c in range(4):
                      P.op("pe", lambda e, c=c, tb_=tb_: e.transpose(out=tb_.ap[:, c * 128:(c + 1) * 128], in_=K1.ap[:, c * 128:(c + 1) * 128],
                                                                    identity=identb.ap), [K1, identb], [], [tb_])
                  cp("act", K1tok.ap.rearrange("p c k -> p (c k)"), tb_.ap[:, 0:512], [tb_], [K1tok])
                  listB1 = P.end_capture()
                  P.tag = "4B"
                  psO = Tf
                  psS = Tf
                  sbank = [[Gb[0], Gb[0]], [Gb[1], Gb[1]]]
                  P.capture()
                  for c in range(4):
                      st_ = scT[c]
                      for h2 in range(2):
                          sc_ps = sbank[h2][c % 2]
                          r = slice(h2 * 64, (h2 + 1) * 64)
                          mm(sc_ps.ap[0:64, 0:64], [(KD.ap[r, c * 128:c * 128 + 64], QD.ap[r, c * 128:c * 128 + 64])], [KD, QD], [sc_ps])
                          mm(sc_ps.ap[0:64, 64:128], [(KC.ap[r, c, :], QC.ap[r, c, :])], [KC, QC], [], [sc_ps])
                          mm(sc_ps.ap[64:128, 64:128], [(KD.ap[r, c * 128 + 64:c * 128 + 128], QD.ap[r, c * 128 + 64:c * 128 + 128])],
                             [KD, QD], [], [sc_ps])
                          tt("dve", st_.ap[0:64, h2, :], sc_ps.ap[0:64, 0:128], cst.ap[0:64, C_U:C_U + 128], ALU.mult, [sc_ps, cst], [], [st_])
                          tt("dve", st_.ap[64:128, h2, 64:128], sc_ps.ap[64:128, 64:128], cst.ap[64:128, C_U + 64:C_U + 128], ALU.mult,
                             [sc_ps, cst], [], [st_])
                  for c in range(4):
                      for h2 in range(2):
                          r = slice(h2 * 64, (h2 + 1) * 64)
                          mm(psS.ap[r, c * 64:(c + 1) * 64], [(K1tok.ap[:, c, r], Vb.ap[:, c, r])], [K1tok, Vb], [], [psS])
                  for c in range(4):
                      stt("dve", Sf.ap, Sf.ap, E1.ap[:, c * 128 + 127:c * 128 + 128], psS.ap[:, c * 64:(c + 1) * 64], ALU.mult, ALU.add,
                          [Sf, E1, psS], [Sf])
                      cp("pool", Spad[0].ap[0:64, c + 1, :], Sf.ap[0:64, :], [Sf], [], [Spad[0]])
                      cp("pool", Spad[1].ap[64:128, c + 1, :], Sf.ap[64:128, :], [Sf], [], [Spad[1]])
                  listB2a = P.end_capture()
                  P.capture()
                  for c in range(4):
                      for h2 in range(2):
                          r = slice(h2 * 64, (h2 + 1) * 64)
                          mm(psO.ap[r, c * 128:(c + 1) * 128],
                             [(Vb.ap[:, c, h2 * 64:(h2 + 1) * 64], scT[c].ap[:, h2, :]),
                              (Spad[h2].ap[:, c, :], Q1.ap[:, c * 128:(c + 1) * 128])], [Vb, scT[c], Spad[h2], Q1], [], [psO])
                  cp("pool", Spad[0].ap[0:64, 0, :], Spad[0].ap[0:64, 4, :], [Spad[0]], [], [Spad[0]])
                  cp("pool", Spad[1].ap[64:128, 0, :], Spad[1].ap[64:128, 4, :], [Spad[1]], [], [Spad[1]])
                  ts("dve", osb.ap, psO.ap, 0.125, None, ALU.mult, None, [psO], [osb])
                  tt("pool", osq.ap, osb.ap, osb.ap, ALU.mult, [osb], [osq])
                  g = Tf
                  mm(g.ap, [(C(C_BLK), osq.ap)], [cst, osq], [g])
                  act(rtb.ap, g.ap, AF.Sqrt, [g, epsc], [rtb], scale=1.0 / 64, bias=epsc.ap[:, 0:1])
                  P.op("dve", lambda e: e.reciprocal(out=rtb.ap, in_=rtb.ap), [rtb], [rtb])
                  stt("dve", osb.ap, osb.ap, ong, rtb.ap, ALU.mult, ALU.mult, [osb, cols, rtb], [osb])
                  tt("pool", yT.ap[:, 1, :], osb.ap, szb.ap, ALU.mult, [osb, szb], [], [yT])
                  listB2b = P.end_capture()
                  P.tag = "5C"
                  Sall = [Sb[0], Sb[1], Tf0]
                  NS, LA = 3, 2
                  P.capture()
                  items = []
                  for pr in range(2):
                      for h2 in range(2):
                          for kb in range(nkb):
                              items.append((pr, h2, kb))

                  def emit_score(i):
                      pr, h2, kb = items[i]
                      hd = 2 * pr + h2
                      q0 = max(0, kb - 4 * sbi) * 128
                      s_ = Sall[i % NS]
                      mm(s_.ap[:, q0:512], [(KTa[hd].ap[:, kb * 128:(kb + 1) * 128], Qa[hd].ap[:, q0:512])], [KTa[hd], Qa[hd]], [s_])
                      if kb >= 4 * sbi:
                          tt("dve", s_.ap[:, q0:q0 + 128], s_.ap[:, q0:q0 + 128], C(C_NEG), ALU.add, [s_, cst], [], [s_])

                  def emit_rest(i):
                      pr, h2, kb = items[i]
                      hd = 2 * pr + h2
                      q0 = max(0, kb - 4 * sbi) * 128
                      s_ = Sall[i % NS]
                      p_ = pT[i % 4]
                      ob = Ob[h2]
                      lo = pr * 192 + (0 if h2 == 0 else 64)
                      act(p_.ap[:, q0:512], s_.ap[:, q0:512], AF.Exp, [s_, biasQ], [p_], bias=biasQ.ap[:, hd, kb:kb + 1])

                      def fnpv(e, ob=ob, q0=q0, kb=kb, lo=lo, p_=p_, nkb=nkb):
                          return e.matmul(ob.ap[:, q0:512], lhsT=Vst.ap[:, kb, lo:lo + 128], rhs=p_.ap[:, q0:512],
                                          start=(kb == 0), stop=(kb == nkb - 1))
                      P.op("pe", fnpv, [Vst, p_], [ob] if kb == 0 else [], [] if kb == 0 else [ob])

                  def finalize(pr, i_last):
                      cp("dve", X.ap[:, 0, :], Ob[0].ap, [Ob[0]], [], [X])
                      cp("dve", X.ap[:, 1, :], Ob[1].ap, [Ob[1]], [], [X])
                      gW = Sall[i_last % NS]
                      mm(gW.ap[0:64, :], [(C(C_PERM, 64), X.ap[:, 0, :])], [cst, X], [gW])
                      mm(gW.ap[64:128, :], [(cst.ap[:, C_PERM + 64:C_PERM + 128], X.ap[:, 1, :])], [cst, X], [], [gW])
                      P.op("dve", lambda e, gW=gW: e.reciprocal(out=gW.ap, in_=gW.ap), [gW], [gW])
                      tt("dve", X.ap[0:64, 0, :], X.ap[0:64, 0, :], gW.ap[0:64, :], ALU.mult, [X, gW], [], [X])
                      tt("dve", X.ap[64:128, 1, :], X.ap[64:128, 1, :], gW.ap[64:128, :], ALU.mult, [X, gW], [], [X])
                      tt("pool", yT.ap[0:64, 2 + pr, :], X.ap[0:64, 0, :], szc[pr].ap[0:64, :], ALU.mult, [X, szc[pr]], [], [yT])
                      tt("pool", yT.ap[64:128, 2 + pr, :], X.ap[64:128, 1, :], szc[pr].ap[64:128, :], ALU.mult, [X, szc[pr]], [], [yT])

                  for j in range(min(LA, len(items))):
                      emit_score(j)
                  for i in range(len(items)):
                      if i + LA < len(items):
                          emit_score(i + LA)
                      emit_rest(i)
                      pr, h2, kb = items[i]
                      if h2 == 1 and kb == nkb - 1:
                          finalize(pr, i)
                  listC2 = P.end_capture()
                  if sbi == 0:
                      P.replay_merged(listFM, pend[0])
                      P.replay_merged(listTM, [])
                  else:
                      P.replay_merged(P.merge_lists(listFM, listTM), pend[0])
                  pend[0] = []
                  if p == 1:
                      P.tag = "6out"
                      P.dma("sp", lambda e, sbi=sbi: e.dma_start(out=yO.ap.rearrange("p a t -> p (a t)"), in_=yh_d[sbi]), [yh_t[sbi]], [yO])
                  listH = []
                  if sbi + 1 < NSB:
                      P.tag = "1hT"
                      if p == 0:
                          P.capture()
                          h_phase(sbi + 1)
                          listH = P.end_capture()
                      else:
                          h_load(sbi + 1)
                  else:
                      nxt = (L, 1) if p == 0 else ((L + 1, 0) if L + 1 < DEPTH else None)
                      if nxt is not None:
                          load_weights(nxt[0], nxt[1], do_win=True, do_wout=False)
                      if p == 0:
                          P.tag = "1hT"
                          h_load(0)
                  P.replay_merged(P.merge_lists(list3, listCs), listH)
                  _h = min(len(listC2), 6)
                  P.replay_merged(listC2[:_h], [])
                  P.replay_merged(listC2[_h:], listB1 + listB2a + listB2b)
                  P.tag = "5C"

                  if dbg:
                      P.op("pool", lambda e: e.tensor_copy(out=X.ap.rearrange("p a t -> p (a t)"), in_=yT.ap[:, 0:2, :].rearrange("p a t -> p (a t)")), [yT], [X])
                      di = (L * 2 + p) * NSB + sbi
                      P.dma("sp", lambda e, di=di: e.dma_start(out=dbg_d[di][:, 0:1024], in_=X.ap.rearrange("p a t -> p (a t)")), [X], [], final=True)
                      P.op("pool", lambda e: e.tensor_copy(out=X.ap.rearrange("p a t -> p (a t)"), in_=yT.ap[:, 2:4, :].rearrange("p a t -> p (a t)")), [yT], [X])
                      P.dma("sp", lambda e, di=di: e.dma_start(out=dbg_d[di][:, 1024:2048], in_=X.ap.rearrange("p a t -> p (a t)")), [X], [], final=True)

                  if stop == 5:
                      raise _Stop()
                  P.tag = "6out"
                  if p == 0:
                      P.dma("sp", lambda e, sbi=sbi: e.dma_start(out=yh_d[sbi], in_=yT.ap.rearrange("p a t -> p (a t)")), [yT], [yh_t[sbi]])
                  else:
                      P.capture()
                      for c in range(4):
                          ci = sbi * 4 + c
                          xb, xnb, xob = xr[c % 2], xn[c % 2], xo[c % 2]
                          rd = [x1_t[ci]] if L > 0 else []
                          P.dma("act", lambda e, xb=xb, ci=ci, xsrc=xsrc: e.dma_start(out=xb.ap, in_=xsrc[ci * 128:(ci + 1) * 128, :]), rd, [xb])
                          for dh in range(2):
                              g = Ob[dh]
                              pairs = []
                              for half, ysrc in ((0, yO), (1, yT)):
                                  for i in range(4):
                                      et = ETILE[half][i]
                                      pairs.append((ysrc.ap[:, i, c * 128:(c + 1) * 128], wout.ap[:, et, dh * 512:(dh + 1) * 512]))
                              mm(g.ap, pairs, [yO, yT, wout], [g])
                              tt("dve", xnb.ap[:, dh * 512:(dh + 1) * 512], g.ap, xb.ap[:, dh * 512:(dh + 1) * 512], ALU.add, [g, xb], [], [xnb])
                          if not last:
                              P.dma("sp", lambda e, xnb=xnb, ci=ci: e.dma_start(out=x1_d[ci * 128:(ci + 1) * 128, :], in_=xnb.ap), [xnb], [x1_t[ci]])
                          else:
                              sq_ = ssqL[c % 2]
                              sc = sq_.ap[:, 0:1]
                              act(junk.ap, xnb.ap, AF.Square, [xnb], [X, sq_], accum_out=sc)
                              act(sq_.ap[:, 1:2], sc, AF.Sqrt, [sq_, epsc], [sq_], scale=1.0 / D, bias=epsc.ap[:, 0:1])
                              P.op("dve", lambda e, sq_=sq_: e.reciprocal(out=sq_.ap[:, 1:2], in_=sq_.ap[:, 1:2]), [sq_], [sq_])
                              stt("dve", xob.ap, xnb.ap, sq_.ap[:, 1:2], gf_bc.ap, ALU.mult, ALU.mult, [xnb, sq_, gf_bc], [xob])
                              P.dma("sp", lambda e, xob=xob, ci=ci: e.dma_start(out=out_d[ci * 128:(ci + 1) * 128, :], in_=xob.ap), [xob], [], final=True)
                      pend[0] = P.end_capture()
              P.replay_merged(pend[0], [])
              pend[0] = []
    try:
        main_loops()
    except _Stop:
        pass
    P.emit()
    nc._namemap = P.namemap
    return nc


def _prep_shared(norm_g, w_in, w_out, gmlp_ln_g, gmlp_ln_b, gmlp_w_s, gmlp_b_s,
                 hgrn_lb, hgrn_onorm_g, fox_b_f, final_norm_g):
    DEPTH = w_in.shape[0]
    f = np.float32
    o = {}
    A_U, A_V, A_Z, B_Q, B_FL, B_I, B_Z, C_Q, C_K, C_V, C_Z, C_FL = 0, 256, 512, 768, 1024, 1280, 1536, 1792, 2304, 2816, 3328, 3840
    win = np.empty((DEPTH, 2, 1024, WCOLS), f)
    for p in range(2):
        a = slice(p * 128, (p + 1) * 128)
        c = slice(p * 256, (p + 1) * 256)
        idx = np.concatenate([
            np.arange(A_U, A_U + 256)[a], np.arange(A_Z, A_Z + 256)[a],
            np.arange(B_Q, B_Q + 256)[a], np.arange(B_FL, B_FL + 256)[a], np.arange(B_Z, B_Z + 256)[a],
            np.arange(C_Q, C_Q + 512)[c], np.arange(C_K, C_K + 512)[c], np.arange(C_Z, C_Z + 512)[c],
            np.arange(A_V, A_V + 256)[a], np.arange(B_I, B_I + 256)[a], np.arange(C_V, C_V + 512)[c],
            np.arange(C_FL, C_FL + 8)[p * 4:(p + 1) * 4]])
        assert idx.size == WCOLS
        win[:, p] = w_in[:, :, idx]
    o["w_in"] = np.ascontiguousarray(win.reshape(DEPTH * 2 * 1024, WCOLS))
    o["w_out"] = np.ascontiguousarray(w_out.reshape(DEPTH * 1024, 1024).astype(f))
    o["consts"] = _consts()
    cols = np.zeros((128, 8 * DEPTH), f)
    for L in range(DEPTH):
        cols[:, 8 * L + 0] = hgrn_lb[L, 0:128]
        cols[:, 8 * L + 1] = hgrn_lb[L, 128:256]
        cols[:, 8 * L + 2] = np.tile(hgrn_onorm_g[L], 2)
    o["cols"] = cols
    o["norm_g"] = np.ascontiguousarray(norm_g.astype(f))
    o["fin_g"] = np.ascontiguousarray(final_norm_g.reshape(1, 1024).astype(f))
    o["ln_g"] = np.ascontiguousarray(gmlp_ln_g.reshape(DEPTH * 2, 128).astype(f))
    o["ln_b"] = np.ascontiguousarray(gmlp_ln_b.reshape(DEPTH * 2, 128).astype(f))
    o["ws_t"] = np.ascontiguousarray(np.transpose(gmlp_w_s, (0, 1, 3, 2)).reshape(DEPTH * 2, 2, 128, 128).astype(f))
    o["b_s"] = np.ascontiguousarray(gmlp_b_s.reshape(DEPTH * 4, 128).astype(f))
    o["b_f"] = np.ascontiguousarray(fox_b_f.reshape(DEPTH * 2, 4).astype(f))
    return o


_CACHE = {}


def kernel(x, norm_g, w_in, w_out, gmlp_ln_g, gmlp_ln_b, gmlp_w_s, gmlp_b_s,
           hgrn_lb, hgrn_onorm_g, fox_b_f, final_norm_g):
    x = np.asarray(x, np.float32)
    B, SEQ, D = x.shape
    DEPTH = w_in.shape[0]
    shared = _prep_shared(*[np.asarray(a, np.float32) for a in (norm_g, w_in, w_out, gmlp_ln_g, gmlp_ln_b, gmlp_w_s,
                                                                 gmlp_b_s, hgrn_lb, hgrn_onorm_g, fox_b_f, final_norm_g)])
    nc = build_program(SEQ, DEPTH)
    work = [0, 1, 4, 5]
    idle = {k: np.zeros_like(v) for k, v in shared.items()}
    idle["consts"] = shared["consts"]
    xz = np.zeros((SEQ, D), np.float32)
    in_maps = []
    for core in range(8):
        if core in work:
            m = dict(shared)
            m["x"] = np.ascontiguousarray(x[work.index(core)])
        else:
            m = dict(idle)
            m["x"] = xz
        in_maps.append(m)
    res = run_bass_kernel_spmd(nc, in_maps, core_ids=list(range(8)))
    return np.stack([res.results[work[b]]["out"] for b in range(B)], axis=0).astype(np.float32)
```
